# Optimizing a Trainium2 kernel written in Bass

```python
import jax, jax.numpy as jnp
from jax import lax
import numpy as np

D_MODEL = 1024
BATCH = 4
SEQ = 4096
DEPTH = 1
DEC_BATCH = 32
DEC_SEQ = 2048
PAST_LEN = 128

HEAD_DIM = 64
N_HEADS = D_MODEL // HEAD_DIM
NA_HEADS = N_HEADS // 2
DIL_HEADS = N_HEADS - NA_HEADS
NA_WIDTH = NA_HEADS * HEAD_DIM
DIL_WIDTH = DIL_HEADS * HEAD_DIM
QKV_WIDTH = 3 * (NA_WIDTH + DIL_WIDTH)
D_FF = -(-8 * D_MODEL // (3 * 256)) * 256
GRID_W = 64
NA_ROWS = 8
NA_COLS = 16
NA_KEY_COLS = 2 * NA_COLS
DIL_PATTERNS = ((128, 1), (512, 4), (2048, 16))
RMS_EPS = 1e-6
NEG = -1e30

kernel_name = "hymba_style_natten_dilated_encoder"


def rms_norm(x, g):
    xf = x.astype(jnp.float32)
    y = xf * lax.rsqrt(jnp.mean(xf * xf, axis=-1, keepdims=True) + RMS_EPS)
    return (y * g.astype(jnp.float32)).astype(x.dtype)


def alibi_slopes(n):
    return np.array([2.0 ** (-8.0 * (i + 1) / n) for i in range(n)], dtype=np.float32)


def neighbourhood_attention(q, k, v, rpb):
    b, t, h, dh = q.shape
    rows = t // GRID_W
    kh = min(NA_ROWS, rows)
    nblk = GRID_W // NA_COLS
    qcol = np.arange(GRID_W).reshape(nblk, NA_COLS)
    kstart = np.clip(np.arange(nblk) * NA_COLS - NA_COLS // 2, 0, GRID_W - NA_KEY_COLS)
    kcol = kstart[:, None] + np.arange(NA_KEY_COLS)
    wstart = np.clip(qcol - NA_COLS // 2, 0, GRID_W - NA_COLS)
    kc3 = kcol[:, None, :]
    col_ok = (kc3 >= wstart[..., None]) & (kc3 < wstart[..., None] + NA_COLS)
    rel_c_idx = np.clip(kc3 - qcol[..., None] + NA_COLS - 1, 0, 2 * NA_COLS - 2)
    col_bias = rpb.astype(jnp.float32)[:, :, rel_c_idx]
    mask = jnp.asarray(col_ok)[:, None, :, None, :]

    qg = (q * (dh ** -0.5)).reshape(b, rows, nblk, NA_COLS, h, dh).transpose(1, 0, 2, 3, 4, 5)
    kg = k.reshape(b, rows, GRID_W, h, dh)[:, :, kcol]
    vg = v.reshape(b, rows, GRID_W, h, dh)[:, :, kcol]

    def row_fn(args):
        r, q_r = args
        rs = jnp.clip(r - kh // 2, 0, rows - kh)
        k_r = lax.dynamic_slice_in_dim(kg, rs, kh, axis=1)
        v_r = lax.dynamic_slice_in_dim(vg, rs, kh, axis=1)
        roff = rs + jnp.arange(kh) - r + NA_ROWS - 1
        bias = col_bias[:, roff].transpose(2, 0, 3, 1, 4)
        s = jnp.einsum('bnqhd,bknjhd->bnhqkj', q_r, k_r).astype(jnp.float32)
        s = jnp.where(mask, s + bias, NEG)
        p = jax.nn.softmax(s, axis=(-2, -1))
        o = jnp.einsum('bnhqkj,bknjhd->bnqhd', p.astype(v.dtype), v_r)
        return o.reshape(b, GRID_W, h, dh)

    out = lax.map(row_fn, (jnp.arange(rows), qg))
    return out.transpose(1, 0, 2, 3, 4).reshape(b, t, h, dh)


def dilated_branch(q, k, v, slopes, window, dilation):
    b, t, h, dh = q.shape
    half = window // (2 * dilation)
    L = t // dilation
    c = half
    nb = -(-L // c)
    lp = nb * c

    def phase(x):
        return x.reshape(b, L, dilation, h, dh).transpose(0, 2, 1, 3, 4)

    def band(x):
        xp = jnp.pad(phase(x), ((0, 0), (0, 0), (c, lp - L + c), (0, 0), (0, 0)))
        xp = xp.reshape(b, dilation, nb + 2, c, h, dh)
        return jnp.concatenate([xp[:, :, :-2], xp[:, :, 1:-1], xp[:, :, 2:]], axis=3)

    qp = jnp.pad(phase(q * (dh ** -0.5)), ((0, 0), (0, 0), (0, lp - L), (0, 0), (0, 0)))
    qp = qp.reshape(b, dilation, nb, c, h, dh)
    kb, vb = band(k), band(v)
    diff = np.arange(3 * c)[None, :] - c - np.arange(c)[:, None]
    lk = np.arange(nb)[:, None] * c - c + np.arange(3 * c)[None, :]
    valid = (np.abs(diff)[None] <= half) & (lk[:, None, :] >= 0) & (lk[:, None, :] < L)
    bias = -slopes[:, None, None] * jnp.asarray((dilation * np.abs(diff)).astype(np.float32))
    s = jnp.einsum('bpnqhd,bpnkhd->bpnhqk', qp, kb).astype(jnp.float32)
    s = jnp.where(jnp.asarray(valid)[:, None], s + bias, NEG)
    m = s.max(-1)
    e = jnp.exp(s - m[..., None])
    den = e.sum(-1)
    num = jnp.einsum('bpnhqk,bpnkhd->bpnqhd', e, vb.astype(jnp.float32))

    def unphase(x):
        x = x.reshape((b, dilation, lp) + x.shape[4:])[:, :, :L]
        x = jnp.swapaxes(x, 1, 2)
        return x.reshape((b, t) + x.shape[3:])

    return (unphase(m.transpose(0, 1, 2, 4, 3)), unphase(num), unphase(den.transpose(0, 1, 2, 4, 3)))


def dilated_attention(q, k, v, slopes):
    branches = [dilated_branch(q, k, v, slopes, w, d) for (w, d) in DIL_PATTERNS]
    m_all = jnp.stack([br[0] for br in branches])
    wts = jnp.exp(m_all - m_all.max(0))
    num = sum(wts[i][..., None] * branches[i][1] for i in range(len(branches)))
    den = sum(wts[i] * branches[i][2] for i in range(len(branches)))
    return num / den[..., None]


def encoder_layer(x, w_in, rpb, g_attn, g_na, g_dil, w_out, g_ffn, w_gate, w_up, w_down):
    b, t, _ = x.shape
    hn = rms_norm(x, g_attn)
    proj = hn @ w_in
    qa, ka, va, qd, kd, vd = jnp.split(proj, 6, axis=-1)
    heads = lambda z: z.reshape(b, t, -1, HEAD_DIM)
    slopes = jnp.asarray(alibi_slopes(DIL_HEADS))
    oa = neighbourhood_attention(heads(qa), heads(ka), heads(va), rpb).reshape(b, t, NA_WIDTH)
    od = dilated_attention(heads(qd), heads(kd), heads(vd), slopes).astype(x.dtype).reshape(b, t, DIL_WIDTH)
    mix = jnp.concatenate([rms_norm(oa, g_na), rms_norm(od, g_dil)], axis=-1)
    x = x + mix @ w_out
    hn = rms_norm(x, g_ffn)
    return x + (jax.nn.silu(hn @ w_gate) * (hn @ w_up)) @ w_down


def setup_inputs(seed: int = 0) -> dict:
    key = jax.random.key(seed)
    ks = jax.random.split(key, 14)
    nrm = lambda k_, shp, sc: jax.random.normal(k_, shp, jnp.float32) * sc
    gain = lambda k_, n: 1.0 + 0.01 * jax.random.normal(k_, (DEPTH, n), jnp.float32)
    return {
        "x_prompt": jax.random.normal(ks[0], (BATCH, SEQ, D_MODEL), jnp.float32),
        "x_sample": jax.random.normal(ks[1], (DEC_BATCH, DEC_SEQ, D_MODEL), jnp.float32),
        "w_in": nrm(ks[2], (DEPTH, D_MODEL, QKV_WIDTH), D_MODEL ** -0.5),
        "rpb": nrm(ks[3], (DEPTH, NA_HEADS, 2 * NA_ROWS - 1, 2 * NA_COLS - 1), 0.02),
        "g_attn": gain(ks[4], D_MODEL),
        "g_na": gain(ks[5], NA_WIDTH),
        "g_dil": gain(ks[6], DIL_WIDTH),
        "w_out": nrm(ks[7], (DEPTH, D_MODEL, D_MODEL), D_MODEL ** -0.5),
        "g_ffn": gain(ks[8], D_MODEL),
        "w_gate": nrm(ks[9], (DEPTH, D_MODEL, D_FF), D_MODEL ** -0.5),
        "w_up": nrm(ks[10], (DEPTH, D_MODEL, D_FF), D_MODEL ** -0.5),
        "w_down": nrm(ks[11], (DEPTH, D_FF, D_MODEL), D_FF ** -0.5),
        "g_final": 1.0 + 0.01 * jax.random.normal(ks[12], (D_MODEL,), jnp.float32),
    }


def reference(x_prompt, x_sample, w_in, rpb, g_attn, g_na, g_dil, w_out, g_ffn, w_gate, w_up, w_down, g_final):
    def trunk(x):
        for l in range(DEPTH):
            x = encoder_layer(x, w_in[l], rpb[l], g_attn[l], g_na[l], g_dil[l], w_out[l],
                              g_ffn[l], w_gate[l], w_up[l], w_down[l])
        return rms_norm(x, g_final)

    y_prompt = trunk(x_prompt)
    y_sample = trunk(x_sample)
    return (y_prompt, y_sample)
```

```python
import contextlib
import numpy as np
import concourse.bass as bass
import concourse.mybir as mybir
from concourse.bass_utils import run_bass_kernel_spmd

F32 = mybir.dt.float32
BF16 = mybir.dt.bfloat16
AF = mybir.ActivationFunctionType
ALU = mybir.AluOpType

D = 1024
DFF = 2816
NJ = DFF // 128
UT = 2048
HALO = 1024
NEGB = -30000.0
EPS = 1e-6
N_DIL_T = 17
NPW = 7
NPE = 4
NHB = 3
NXT = 3
NA_INT = 5


def na_special_pairs(kind):
    if kind == "s":
        return [(0, k) for k in range(4)] + [(1, k) for k in range(4)] + \
               [(14, k) for k in range(12, 16)] + [(15, k) for k in range(12, 16)]
    return [(0, k) for k in (0, 1, 2, 3, 22, 23)] + [(1, k) for k in (0, 1, 2, 3, 23)] + \
           [(14, k) for k in (12, 13, 14, 15, 16)] + [(15, k) for k in (12, 13, 14, 15, 16, 17)]


def na_ntypes(kind):
    return NA_INT + len(na_special_pairs(kind))


def na_steps(kind):
    sp = na_special_pairs(kind)
    sp_idx = {p: NA_INT + i for i, p in enumerate(sp)}
    nslots = 16 if kind == "s" else 24
    chunks = []
    for j0 in range(0, 16, 4):
        steps = []
        js = list(range(j0, j0 + 4))
        for j in js:
            if j in (0, 1, 14, 15):
                for (jj, k) in sp:
                    if jj == j:
                        steps.append((k, j, j + 1, sp_idx[(j, k)], None))
        interior = [j for j in js if j not in (0, 1, 14, 15)]
        if interior:
            for k in range(nslots):
                v = [j for j in interior if abs(k - j) <= 2 and k < 16]
                if v:
                    ja, jb = v[0], v[-1] + 1
                    steps.append((k, ja, jb, 2 - (k - ja), None))
        chunks.append((j0, steps))
    return chunks


def dil_steps(kind):
    chunks = []
    for j0 in range(0, 16, 4):
        steps = []
        js = list(range(j0, j0 + 4))
        for k in range(16):
            v = [j for j in js if abs(k - j) <= 8]
            if v:
                ja, jb = v[0], v[-1] + 1
                steps.append((k, ja, jb, 8 - (k - ja), None))
        if kind == "p":
            for s in range(16, 24):
                v = [j for j in js if j >= 8 and s <= j + 8]
                if v:
                    ja, jb = v[0], v[-1] + 1
                    steps.append((s, ja, jb, 8 - s + ja, 0))
                v = [j for j in js if j <= 7 and s >= j + 16]
                if v:
                    ja, jb = v[0], v[-1] + 1
                    steps.append((s, ja, jb, 32 - s + ja, 1))
        chunks.append((j0, steps))
    return chunks


def host_dil_table():
    slopes = np.array([2.0 ** (-(i + 1)) for i in range(8)], dtype=np.float64)
    k = np.arange(128)[:, None, None]
    i = np.arange(N_DIL_T)[None, :, None]
    q = np.arange(128)[None, None, :]
    delta = 128 * (8 - i) + k - q
    ad = np.abs(delta)
    mult = (ad <= 64).astype(np.float64) + ((delta % 4 == 0) & (ad <= 256)) + ((delta % 16 == 0) & (ad <= 1024))
    out = np.empty((8, 128, N_DIL_T, 128), dtype=np.float32)
    for h in range(8):
        out[h] = (mult * np.exp(-slopes[h] * ad)).astype(np.float32)
    return out.reshape(8, 128, N_DIL_T * 128)


def _na_tile(rpb, R, qrows, krows):
    out = np.full((8, 128, 128), NEGB, dtype=np.float32)
    kc = np.arange(64)[:, None]
    qc = np.arange(64)[None, :]
    ws = np.clip(qc - 8, 0, 48)
    ok = (kc >= ws) & (kc < ws + 16)
    rel = np.clip(kc - qc + 15, 0, 30)
    for ki, kr in enumerate(krows):
        for qi, qr in enumerate(qrows):
            if kr is None or kr < 0 or kr >= R:
                continue
            rs = int(np.clip(qr - 4, 0, R - 8))
            if not (rs <= kr < rs + 8):
                continue
            roff = kr - qr + 7
            g = rpb[:, roff][:, rel]
            out[:, ki * 64:(ki + 1) * 64, qi * 64:(qi + 1) * 64] = np.where(ok[None], g, np.float32(NEGB))
    return out


def host_na_table(rpb, kind, half):
    nt = na_ntypes(kind)
    out = np.empty((8, 128, nt, 128), dtype=np.float32)
    for i in range(NA_INT):
        dlt = 2 - i
        out[:, :, i, :] = _na_tile(rpb, 64, (16, 17), (16 + 2 * dlt, 17 + 2 * dlt))
    for n, (j, slot) in enumerate(na_special_pairs(kind)):
        if kind == "s":
            R = 32
            qrows = (2 * j, 2 * j + 1)
            krows = (2 * slot, 2 * slot + 1)
        else:
            R = 64
            if half == 0:
                qrows = (2 * j, 2 * j + 1)
                krows = (2 * slot, 2 * slot + 1)
            else:
                qrows = (32 + 2 * j, 33 + 2 * j)
                if slot < 16:
                    krows = (32 + 2 * slot, 33 + 2 * slot)
                else:
                    g = slot - 8
                    krows = (2 * g, 2 * g + 1)
        out[:, :, NA_INT + n, :] = _na_tile(rpb, R, qrows, krows)
    return out.reshape(8, 128, nt * 128)


class Buf:
    __slots__ = ("name", "w", "r", "dsem", "dcnt", "dma")

    def __init__(self, name):
        self.name = name
        self.w = None
        self.r = {}
        self.dsem = {}
        self.dcnt = {}
        self.dma = False


class Ctx:
    def __init__(self, nc, es):
        self.nc = nc
        self.es = es
        self.eng = {"pe": nc.tensor, "act": nc.scalar, "dve": nc.vector, "pool": nc.gpsimd, "sp": nc.sync}
        self.sem = {e: es.enter_context(nc.semaphore("s_" + e)) for e in self.eng}
        self.cnt = {e: 0 for e in self.eng}
        self.seen = {e: {} for e in self.eng}
        self.semname = {}
        self.dma_toks = []
        self.nsem = 0

    def buf(self, name, dma=False):
        b = Buf(name)
        b.dma = dma
        return b

    def _wait(self, e, tok, raw):
        if tok is None:
            return
        te, sem, val = tok
        if te == e:
            if e == "pe" or e == "sp" or (not raw and e != "pool"):
                return
        key = id(sem)
        if self.seen[e].get(key, 0) >= val:
            return
        self.seen[e][key] = val
        self.eng[e].wait_ge(sem, val)

    def _sync(self, e, reads, writes):
        for b in reads:
            self._wait(e, b.w, True)
        for b in writes:
            self._wait(e, b.w, False)
            for t in b.r.values():
                self._wait(e, t, False)

    def _mark(self, tok, reads, writes):
        for b in reads:
            b.r[id(tok[1])] = tok
        for b in writes:
            b.w = tok
            b.r = {}

    def op(self, e, fns, reads=(), writes=()):
        self._sync(e, reads, writes)
        if not isinstance(fns, (list, tuple)):
            fns = [fns]
        ins = None
        for f in fns:
            ins = f()
        self.cnt[e] += 1
        ins.then_inc(self.sem[e], 1)
        tok = (e, self.sem[e], self.cnt[e])
        self._mark(tok, reads, writes)
        return tok

    def dma(self, q, out_ap, in_ap, reads, writes, sembuf, waw=True, track=True):
        if waw:
            self._sync(q, reads, writes)
        else:
            self._sync(q, reads, [])
        cls = "sw" if q == "pool" else "hw"
        if cls not in sembuf.dsem:
            sembuf.dsem[cls] = self.es.enter_context(self.nc.semaphore("d%s_%s_%d" % (cls, sembuf.name, self.nsem)))
            sembuf.dcnt[cls] = 0
            self.nsem += 1
        sem = sembuf.dsem[cls]
        self.eng[q].dma_start(out=out_ap, in_=in_ap).then_inc(sem, 16)
        sembuf.dcnt[cls] += 16
        tok = ("dma", sem, sembuf.dcnt[cls])
        self._mark(tok, reads, writes)
        if track:
            self.dma_toks.append(tok)
        return tok

    def barrier(self):
        toks = [(e, self.sem[e], self.cnt[e]) for e in self.eng if self.cnt[e] > 0]
        last = {}
        for t in self.dma_toks:
            k = id(t[1])
            if k not in last or last[k][2] < t[2]:
                last[k] = t
        toks += list(last.values())
        self.dma_toks = list(last.values())
        for e in self.eng:
            for t in toks:
                if t[0] == e:
                    continue
                key = id(t[1])
                if self.seen[e].get(key, 0) >= t[2]:
                    continue
                self.seen[e][key] = t[2]
                self.eng[e].wait_ge(t[1], t[2])


def build_program(n_sample=4, do_prompt=True, debug=False):
    nc = bass.Bass("TRN2", target_bir_lowering=False)
    units = ["s"] * n_sample + (["p"] if do_prompt else [])
    nunits = len(units)
    n_in_rows = nunits * UT + (HALO if do_prompt else 0)

    def din(name, shape, dt=F32):
        return nc.dram_tensor(name, list(shape), dt, kind="ExternalInput").ap()

    def dscr(name, shape, dt=BF16):
        return nc.dram_tensor(name, list(shape), dt, kind="Internal").ap()

    xin = din("xin", [n_in_rows, D])
    w_in = din("w_in", [D, 3 * D])
    w_out = din("w_out", [D, D])
    w_gate = din("w_gate", [D, DFF])
    w_up = din("w_up", [D, DFF])
    w_down = din("w_down", [DFF, D])
    gv = din("gv", [4, D])
    bdil = din("bdil", [8, 128, N_DIL_T * 128])
    nts, ntp = na_ntypes("s"), na_ntypes("p")
    bna_s = din("bna_s", [8, 128, nts * 128])
    bna_p = din("bna_p", [8, 128, ntp * 128])
    gates_in = din("gates", [128, 2])
    yout = nc.dram_tensor("yout", [nunits * UT, D], F32, kind="ExternalOutput").ap()

    w_in_b = dscr("w_in_b", [D, 3 * D])
    w_out_b = dscr("w_out_b", [D, D])
    w_gate_b = dscr("w_gate_b", [D, DFF])
    w_up_b = dscr("w_up_b", [D, DFF])
    w_down_b = dscr("w_down_b", [DFF, D])
    wdil = dscr("wdil", [8, 128, N_DIL_T * 128])
    wna_s = dscr("wna_s", [8, 128, nts * 128])
    wna_p = dscr("wna_p", [8, 128, ntp * 128])
    NTMAX = max(nts, ntp)

    es = contextlib.ExitStack()
    with es:
        cx = Ctx(nc, es)
        pe, act, dve, pool, sp = nc.tensor, nc.scalar, nc.vector, nc.gpsimd, nc.sync

        def sb(name, shape, dt):
            return es.enter_context(nc.sbuf_tensor(name, list(shape), dt))

        def ps(name, shape, dt):
            return es.enter_context(nc.psum_tensor(name, list(shape), dt))

        gb = sb("gb", [128, 4, D], F32)
        ident = sb("ident", [128, 128], BF16)
        identf = sb("identf", [128, 128], F32)
        gates = sb("gates_sb", [128, 2], F32)
        xt = [sb("xt%d" % i, [128, D], F32) for i in range(NXT)]
        hb = [sb("hb%d" % i, [128, D], BF16) for i in range(NHB)]
        o_sb = sb("o_sb", [128, 16, D], BF16)
        junk = sb("junk", [128, D], BF16)
        yt = [sb("yt%d" % i, [128, D], F32) for i in range(2)]
        stat = sb("stat", [128, 64], F32)
        eps_t = sb("eps_t", [128, 1], F32)
        rden = [sb("rden%d" % i, [128, 4], F32) for i in range(2)]
        REG_ELEMS = 65536
        reg = sb("region", [128, REG_ELEMS], BF16)
        regpos = [0]

        def carve(shape_free, dt):
            n = int(np.prod(shape_free))
            ne = n * (2 if dt == F32 else 1)
            off = regpos[0]
            off = (off + 1) // 2 * 2
            assert off + ne <= REG_ELEMS, (off, ne)
            regpos[0] = off + ne
            ap = reg[:, off:off + ne]
            if dt == F32:
                ap = ap.bitcast(F32)
            if len(shape_free) == 2:
                ap = ap.rearrange("p (a b) -> p a b", a=shape_free[0])
            elif len(shape_free) == 3:
                ap = ap.rearrange("p (a b c) -> p a b c", a=shape_free[0], b=shape_free[1])
            return ap

        regpos[0] = 0
        hnT = carve([8, UT + HALO], BF16)
        Vflat = carve([24 * 8, 65], BF16)
        Vsb = Vflat.rearrange("p (a b) c -> p a b c", a=24)
        qtm_off = (regpos[0] + 1) // 2 * 2
        QTm = [carve([UT], BF16) for _ in range(2)]
        KT1 = carve([UT + HALO], BF16)
        kt_end = regpos[0]
        wv_sb = carve([8, 512], BF16)
        wq_sb = [carve([8, 128], BF16) for _ in range(2)]
        wk_sb = [carve([8, 128], BF16) for _ in range(2)]
        wtab = [carve([NTMAX * 128], BF16) for _ in range(2)]
        pexp = [carve([512], BF16) for _ in range(NPE)]
        pw = [carve([512], BF16) for _ in range(NPW)]
        endA = regpos[0]
        regpos[0] = 0
        hnT_f32 = carve([8, (UT + HALO) // 2], F32)
        regpos[0] = qtm_off
        btmp = carve([NTMAX * 128], F32)
        assert regpos[0] <= kt_end
        regpos[0] = 0
        wout_sb = carve([8, D], BF16)
        wd_sb = carve([NJ, D], BF16)
        x1 = carve([4, D], F32)
        mixT = carve([8, 512], BF16)
        hn2T = carve([8, 512], BF16)
        hT = carve([NJ, 512], BF16)
        wg_sb = [carve([8, 128], BF16) for _ in range(2)]
        wu_sb = [carve([8, 128], BF16) for _ in range(2)]
        sg = [carve([512], F32) for _ in range(2)]
        endB = regpos[0]

        NMM = 6
        mm = [ps("mm%d" % i, [128, 512], F32) for i in range(NMM)]
        tp = [m[:, :].bitcast(BF16).rearrange("p (c n) -> p c n", c=8) for m in mm]
        pv = [ps("pv%d" % i, [128, 4, 128], F32) for i in range(2)]

        B = cx.buf
        b_gb = B("gb", dma=True)
        b_ident = B("ident")
        b_gates = B("gates", dma=True)
        b_xt = [B("xt%d" % i, dma=True) for i in range(NXT)]
        b_hb = [B("hb%d" % i) for i in range(NHB)]
        b_o = [B("o%d" % i) for i in range(16)]
        b_junk = B("junk")
        b_yt = [B("yt%d" % i, dma=True) for i in range(2)]
        b_stats = [B("stat%d" % i) for i in range(8)]
        b_rden = [B("rden%d" % i) for i in range(2)]
        b_hnT = [B("hnT%d" % i) for i in range(6)]
        b_V = [B("V%d" % i) for i in range(24)]
        b_QTm = [B("QTm%d" % i) for i in range(2)]
        b_KT1 = B("KT1")
        b_wv = B("wv", dma=True)
        b_wq = [B("wq%d" % i, dma=True) for i in range(2)]
        b_wk = [B("wk%d" % i, dma=True) for i in range(2)]
        b_wtab = [B("wtab%d" % i, dma=True) for i in range(2)]
        b_pexp = [B("pexp%d" % i) for i in range(NPE)]
        b_pw = [B("pw%d" % i) for i in range(NPW)]
        b_tabld = B("tabld", dma=True)
        b_eps = B("eps")
        b_wout = B("wout", dma=True)
        b_wd = B("wd", dma=True)
        b_x1 = [B("x1_%d" % i) for i in range(4)]
        b_mixT = B("mixT")
        b_hn2T = B("hn2T")
        b_hT = [B("hT%d" % i) for i in range(NJ)]
        b_wg = [B("wg%d" % i, dma=True) for i in range(2)]
        b_wu = [B("wu%d" % i, dma=True) for i in range(2)]
        b_sg = [B("sg%d" % i) for i in range(2)]
        b_mm = [B("mm%d" % i) for i in range(NMM)]
        b_tp = b_mm
        b_pv = [B("pv%d" % i) for i in range(2)]
        b_scr = {n: B(n, dma=True) for n in ("w_out_b", "w_gate_b", "w_up_b", "w_down_b")}
        b_win = [B("w_in_b%d" % i, dma=True) for i in range(6)]
        gate_t = sb("gate_t", [128, 2], F32)
        b_gate_t = B("gate_t")
        b_wdil = [B("wdil%d" % h, dma=True) for h in range(8)]
        b_wnas = [B("wnas%d" % h, dma=True) for h in range(8)]
        b_wnap = [B("wnap%d" % h, dma=True) for h in range(8)]

        rr = {"mm": 0, "tp": 0, "xt": 0, "hb": 0, "yt": 0, "pe": 0, "pw": 0, "pv": 0, "rden": 0, "wt": 0,
              "qk": 0, "wgu": 0, "sg": 0, "st": 0}

        def nxt(key, n):
            i = rr[key]
            rr[key] = (i + 1) % n
            return i

        def stat_slot():
            i = nxt("st", 8)
            return i * 8

        def cast_weight(name, src, dst):
            nr = src.shape[0]
            for r0 in range(0, nr, 256):
                r1 = min(nr, r0 + 256)
                cx.dma("pool", dst[r0:r1, :], src[r0:r1, :], [], [b_scr[name]], b_scr[name], waw=False, track=False)

        def cast_win_block(i):
            for r0 in range(0, D, 256):
                cx.dma("pool", w_in_b[r0:r0 + 256, i * 512:(i + 1) * 512], w_in[r0:r0 + 256, i * 512:(i + 1) * 512],
                       [], [b_win[i]], b_win[i], waw=False, track=False)

        def pool_gate(bufs):
            cx.op("pool", lambda: pool.memset(gate_t[:], 0.0), bufs, [b_gate_t])

        for i in (2, 0, 1):
            cast_win_block(i)
        for i in range(4):
            cx.dma("sp", gb[:, i, :], gv[i:i + 1, :].partition_broadcast(128), [], [b_gb], b_gb, waw=False)
        cx.dma("sp", gates[:], gates_in[:, :], [], [b_gates], b_gates)
        cx.op("pool", lambda: pool.memset(identf[:], 1.0), [], [b_ident])
        cx.op("pool", lambda: pool.affine_select(out=identf[:], in_=identf[:], pattern=[[-1, 128]],
                                                compare_op=ALU.is_equal, fill=0.0, base=0, channel_multiplier=1),
              [b_ident], [b_ident])
        cx.op("dve", lambda: dve.tensor_copy(ident[:], identf[:]), [b_ident], [b_ident])
        cx.op("dve", lambda: dve.memset(eps_t[:], EPS), [], [b_eps])

        b_alias = [b_QTm[0], b_QTm[1], b_KT1]

        def build_table_head(src, dst, bl, ncols, h):
            cx.dma("sp", btmp[:, 0:ncols], src[h], [], b_alias, b_tabld)
            wb = wtab[h % 2]
            cx.op("act", (lambda wb=wb: act.activation(out=wb[:, 0:ncols], in_=btmp[:, 0:ncols], func=AF.Exp)),
                  b_alias, [b_wtab[h % 2]])
            cx.dma("sp", dst[h], wb[:, 0:ncols], [b_wtab[h % 2]], [bl[h]], bl[h])

        built = {"s": False, "p": False, "casts": False}
        cast_jobs = []
        cast_gate = [None]
        def rms_sd(src_ap, b_src, ncols, st0):
            b_stat = b_stats[st0 // 8]
            cx.op("act", lambda: act.activation(out=junk[:, 0:ncols], in_=src_ap, func=AF.Square,
                                                accum_out=stat[:, st0:st0 + 1]),
                  [b_src], [b_junk, b_stat])
            cx.op("act", lambda: act.activation(out=stat[:, st0 + 1:st0 + 2], in_=stat[:, st0:st0 + 1],
                                                func=AF.Sqrt, scale=1.0 / ncols, bias=eps_t[:, 0:1]),
                  [b_stat, b_eps], [b_stat])
            return stat[:, st0 + 1:st0 + 2]

        def rms_apply(out_ap, b_out, src_ap, b_src, sd, st0, g_ap):
            b_stat = b_stats[st0 // 8]
            cx.op("dve", lambda: dve.reciprocal(out=stat[:, st0 + 2:st0 + 3], in_=sd), [b_stat], [b_stat])
            cx.op("dve", lambda: dve.scalar_tensor_tensor(out=out_ap, in0=src_ap, scalar=stat[:, st0 + 2:st0 + 3], in1=g_ap,
                                                          op0=ALU.mult, op1=ALU.mult),
                  [b_src, b_stat, b_gb], [b_out])

        def transpose_to(src_bf, b_src, dstT, b_dst, col0):
            ti = nxt("mm", NMM)
            cx.op("pe", [(lambda c=c: pe.transpose(tp[ti][:, c, :], src_bf[:, c * 128:(c + 1) * 128], ident[:]))
                         for c in range(8)],
                  [b_src, b_ident], [b_tp[ti]])
            cx.op("act", lambda: act.copy(out=dstT[:, :, col0:col0 + 128], in_=tp[ti]),
                  [b_tp[ti]], [b_dst])

        def norm_tile_A(src_ap, b_src, gidx):
            st0 = stat_slot()
            sd = rms_sd(src_ap, b_src, D, st0)
            hi = nxt("hb", NHB)
            rms_apply(hb[hi][:], b_hb[hi], src_ap, b_src, sd, st0, gb[:, gidx, :])
            return hi

        def norm_tile_B(hi, dstT, b_dst, col0):
            transpose_to(hb[hi], b_hb[hi], dstT, b_dst, col0)

        def load_w_chunk(dst, b_dst, src_scr, b_src, col0, ncol, q="sp"):
            if src_scr is w_in_b:
                b_src = b_win[col0 // 512]
            cx.dma(q, dst[:, :, 0:ncol], src_scr[:, col0:col0 + ncol].rearrange("(c p) n -> p c n", p=128),
                   [b_src], [b_dst], b_dst)

        def proj_T(w_sb, b_w, srcT, b_srcs, col0, n, dst_ap, b_dst):
            mi = nxt("mm", NMM)
            cx.op("pe", [(lambda kc=kc: pe.matmul(mm[mi][:, 0:n], lhsT=w_sb[:, kc, :], rhs=srcT[:, kc, col0:col0 + n],
                                                  start=(kc == 0), stop=(kc == 7))) for kc in range(8)],
                  [b_w] + b_srcs, [b_mm[mi]])
            cx.op("dve", lambda: dve.tensor_copy(dst_ap, mm[mi][:, 0:n]), [b_mm[mi]], [b_dst])

        PV_DEPTH = 4
        pvq = []

        def pv_emit(ent):
            (k, ja, jb, pwi, pvi, j0, hv, hcol, first, last) = ent
            fns = []
            for n_, j in enumerate(range(ja, jb)):
                st_flag = first and n_ == 0
                fns.append(lambda n_=n_, j=j, st_flag=st_flag: pe.matmul(
                    pv[pvi][:, j - j0, 0:65], lhsT=pw[pwi][:, n_ * 128:(n_ + 1) * 128],
                    rhs=Vsb[:, k, hv, :], start=st_flag, stop=False, skip_group_check=True))
            cx.op("pe", fns, [b_pw[pwi], b_V[k]], [b_pv[pvi]])
            if last:
                ri = nxt("rden", 2)
                cx.op("dve", lambda: dve.reciprocal(out=rden[ri][:], in_=pv[pvi][:, :, 64]),
                      [b_pv[pvi]], [b_rden[ri]])
                cx.op("dve", lambda: dve.tensor_tensor(out=o_sb[:, j0:j0 + 4, hcol:hcol + 64], in0=pv[pvi][:, :, 0:64],
                                                       in1=rden[ri][:].unsqueeze(2).to_broadcast([128, 4, 64]),
                                                       op=ALU.mult),
                      [b_pv[pvi], b_rden[ri]], [b_o[j] for j in range(j0, j0 + 4)])

        def pv_flush(keep=0):
            while len(pvq) > keep:
                pv_emit(pvq.pop(0))

        def attention_head(kind, chunks, qt_ap, kt_ap, b_qt, b_kt, hcol, hv, wt_ap, b_wt):
            for (j0, steps) in chunks:
                pvi = nxt("pv", 2)
                ns = len(steps)
                for si_, (k, ja, jb, ti, gate) in enumerate(steps):
                    n = (jb - ja) * 128
                    mi = nxt("mm", NMM)
                    cx.op("pe", lambda: pe.matmul(mm[mi][:, 0:n], lhsT=kt_ap[:, k * 128:(k + 1) * 128],
                                                  rhs=qt_ap[:, ja * 128:jb * 128], start=True, stop=True),
                          [b_kt, b_qt], [b_mm[mi]])
                    pv_flush(PV_DEPTH)
                    pei = nxt("pe", NPE)
                    cx.op("act", lambda: act.activation(out=pexp[pei][:, 0:n], in_=mm[mi][:, 0:n], func=AF.Exp,
                                                        scale=0.125),
                          [b_mm[mi]], [b_pexp[pei]])
                    pwi = nxt("pw", NPW)
                    if gate is None:
                        cx.op("dve", lambda: dve.tensor_tensor(out=pw[pwi][:, 0:n], in0=pexp[pei][:, 0:n],
                                                               in1=wt_ap[:, ti * 128:ti * 128 + n], op=ALU.mult),
                              [b_pexp[pei], b_wt], [b_pw[pwi]])
                    else:
                        cx.op("dve", lambda: dve.scalar_tensor_tensor(out=pw[pwi][:, 0:n], in0=pexp[pei][:, 0:n],
                                                                      scalar=gates[:, gate:gate + 1],
                                                                      in1=wt_ap[:, ti * 128:ti * 128 + n],
                                                                      op0=ALU.mult, op1=ALU.mult),
                              [b_pexp[pei], b_wt, b_gates], [b_pw[pwi]])
                    pvq.append((k, ja, jb, pwi, pvi, j0, hv, hcol, si_ == 0, si_ == ns - 1))

        steps_cache = {("na", "s"): na_steps("s"), ("na", "p"): na_steps("p"),
                       ("dil", "s"): dil_steps("s"), ("dil", "p"): dil_steps("p")}
        for u, kind in enumerate(units):
            row0 = u * UT
            nkt = 24 if kind == "p" else 16
            ntok = nkt * 128

            def xrows(t):
                if t < 16:
                    return row0 + t * 128
                return nunits * UT + (t - 16) * 128

            def vproj(t):
                mi = nxt("mm", NMM)
                cx.op("pe", [(lambda kc=kc: pe.matmul(mm[mi][:, :], lhsT=hnT[:, kc, t * 128:(t + 1) * 128],
                                                      rhs=wv_sb[:, kc, :], start=(kc == 0), stop=(kc == 7)))
                             for kc in range(8)],
                      [b_wv, b_hnT[t // 4]], [b_mm[mi]])
                cx.op("dve", lambda: dve.tensor_copy(Vsb[:, t, :, 0:64],
                                                     mm[mi][:, :].rearrange("p (h d) -> p h d", h=8)),
                      [b_mm[mi]], [b_V[t]])

            if u > 0:
                cx.barrier()
            cx.op("dve", lambda: dve.memset(Vflat[:, :, 64:65], 1.0), [], b_V)
            wq_ = "pool" if u == 0 else "sp"
            load_w_chunk(wv_sb, b_wv, w_in_b, None, 1024, 512, q=wq_)
            qi0 = nxt("qk", 2)
            load_w_chunk(wq_sb[qi0], b_wq[qi0], w_in_b, None, 0, 128, q=wq_)
            load_w_chunk(wk_sb[qi0], b_wk[qi0], w_in_b, None, 512, 128, q=wq_)
            early_qk = built[kind]

            def zero_q_halves():
                cx.op("pool", lambda: pool.memset(QTm[0][64:128, :], 0.0), [], [b_QTm[0]])
                cx.op("pool", lambda: pool.memset(QTm[1][0:64, :], 0.0), [], [b_QTm[1]])

            def qproj(qi, tq):
                mi = nxt("mm", NMM)
                cx.op("pe", [(lambda kc=kc: pe.matmul(mm[mi][:, :], lhsT=wq_sb[qi][:, kc, :],
                                                      rhs=hnT[:, kc, tq * 512:(tq + 1) * 512],
                                                      start=(kc == 0), stop=(kc == 7))) for kc in range(8)],
                      [b_wq[qi], b_hnT[tq]], [b_mm[mi]])
                cx.op("dve", lambda: dve.tensor_copy(QTm[0][0:64, tq * 512:(tq + 1) * 512], mm[mi][0:64, :]),
                      [b_mm[mi]], [b_QTm[0]])
                cx.op("dve", lambda: dve.tensor_copy(QTm[1][64:128, tq * 512:(tq + 1) * 512], mm[mi][64:128, :]),
                      [b_mm[mi]], [b_QTm[1]])

            def kproj(qi, tk):
                proj_T(wk_sb[qi], b_wk[qi], hnT, [b_hnT[tk]], tk * 512, 512, KT1[:, tk * 512:(tk + 1) * 512], b_KT1)

            if early_qk:
                zero_q_halves()
            def after_tile(t):
                norm_tile_B(prev[0], hnT, b_hnT[t // 4], t * 128)
                vproj(t)
                if early_qk and t % 4 == 3:
                    if t // 4 < 4:
                        qproj(qi0, t // 4)
                    kproj(qi0, t // 4)
                if not built[kind] and t % 2 == 1 and t < 16:
                    if kind == "s":
                        build_table_head(bna_s, wna_s, b_wnas, nts * 128, t // 2)
                    else:
                        build_table_head(bna_p, wna_p, b_wnap, ntp * 128, t // 2)

            prev = None
            for t in range(nkt):
                xi = nxt("xt", NXT)
                r0 = xrows(t)
                cx.dma("sp", xt[xi][:], xin[r0:r0 + 128, :], [], [b_xt[xi]], b_xt[xi])
                hi = norm_tile_A(xt[xi][:], b_xt[xi], 0)
                if prev is not None:
                    after_tile(prev[1])
                prev = (hi, t)
            after_tile(prev[1])
            if not built[kind]:
                last_tab = b_wnas[7] if kind == "s" else b_wnap[7]
                built[kind] = True
                if not built["casts"]:
                    def _rows(name, src, dst, r0, r1):
                        def f():
                            for a in range(r0, r1, 256):
                                b_ = min(r1, a + 256)
                                cx.dma("pool", dst[a:b_, :], src[a:b_, :], [], [b_scr[name]], b_scr[name],
                                       waw=False, track=False)
                        return f

                    def _dil(h0):
                        def f():
                            for h in range(h0, h0 + 4):
                                cx.dma("pool", wdil[h], bdil[h], [], [b_wdil[h]], b_wdil[h], track=False)
                        return f

                    cast_jobs.extend([
                        (lambda: [cast_win_block(i) for i in (5, 3, 4)]),
                        _dil(0), _dil(4),
                        _rows("w_out_b", w_out, w_out_b, 0, D),
                        _rows("w_gate_b", w_gate, w_gate_b, 0, 512), _rows("w_gate_b", w_gate, w_gate_b, 512, D),
                        _rows("w_up_b", w_up, w_up_b, 0, 512), _rows("w_up_b", w_up, w_up_b, 512, D),
                        _rows("w_down_b", w_down, w_down_b, 0, 1408), _rows("w_down_b", w_down, w_down_b, 1408, DFF),
                    ])
                    cast_gate[0] = last_tab
                    built["casts"] = True
            if not early_qk:
                zero_q_halves()
            def release_casts(njobs, gate_bufs):
                if cast_jobs:
                    pool_gate(gate_bufs)
                    for _ in range(njobs):
                        if cast_jobs:
                            cast_jobs.pop(0)()

            if cast_jobs:
                release_casts(1, [cast_gate[0]])

            pairs = [(G, c) for G in range(2) for c in range(4)]
            pair_qi = {0: qi0}

            def load_pair(pi):
                G, c = pairs[pi]
                qi = nxt("qk", 2)
                load_w_chunk(wq_sb[qi], b_wq[qi], w_in_b, None, 1536 * G + c * 128, 128)
                load_w_chunk(wk_sb[qi], b_wk[qi], w_in_b, None, 1536 * G + 512 + c * 128, 128)
                pair_qi[pi] = qi

            head_wt = {}

            def load_wtab(hn):
                G, h = hn // 8, hn % 8
                wi = nxt("wt", 2)
                if G == 0:
                    src, bsrc, ncols = ((wna_s, b_wnas, nts * 128) if kind == "s" else (wna_p, b_wnap, ntp * 128))
                else:
                    src, bsrc, ncols = wdil, b_wdil, N_DIL_T * 128
                cx.dma("sp", wtab[wi][:, 0:ncols], src[h], [bsrc[h]], [b_wtab[wi]], b_wtab[wi])
                head_wt[hn] = wi

            bg_tab = (kind == "s" and not built["p"] and u + 1 < nunits and units[u + 1] == "p")
            PC = ntp * 128 // 16
            stage_f = hnT_f32[:, :, UT // 2:UT // 2 + PC]
            stage_b = hnT[:, :, UT + 2 * PC:UT + 3 * PC]
            b_stage = [b_hnT[4], b_hnT[5]]

            def bg_load(r):
                h, hf_ = r // 2, r % 2
                cx.dma("sp", stage_f, bna_p[h].rearrange("p (a b) -> p a b", a=16)[:, hf_ * 8:(hf_ + 1) * 8, :],
                       [], b_stage, b_tabld)

            def bg_exp_store(r):
                h, hf_ = r // 2, r % 2
                cx.op("act", lambda: act.activation(out=stage_b, in_=stage_f, func=AF.Exp), b_stage, b_stage)
                cx.dma("sp", wna_p[h].rearrange("p (a b) -> p a b", a=16)[:, hf_ * 8:(hf_ + 1) * 8, :], stage_b,
                       b_stage, [b_wnap[h]], b_wnap[h], waw=False)

            def bg_round(r):
                if bg_tab:
                    if r >= 1:
                        bg_exp_store(r - 1)
                    if r < 16:
                        bg_load(r)

            load_wtab(0)
            for pi, (G, c) in enumerate(pairs):
                bg_round(2 * pi)
                cb = 1536 * G
                gname = "na" if G == 0 else "dil"
                chunks = steps_cache[(gname, kind)]
                if G == 1 and c == 0:
                    pv_flush()
                    load_w_chunk(wv_sb, b_wv, w_in_b, None, cb + 1024, 512)
                    for t in range(nkt):
                        vproj(t)
                qi = pair_qi[pi]
                if not (pi == 0 and early_qk):
                    for tq in range(4):
                        qproj(qi, tq)
                    for tk in range(ntok // 512):
                        kproj(qi, tk)
                if pi >= 1:
                    release_casts(1 if pi < 6 else 2, [b_KT1])
                if pi + 1 < len(pairs):
                    load_pair(pi + 1)
                for hl in range(2):
                    hn = 2 * pi + hl
                    h = hn % 8
                    if hn + 1 < 16:
                        load_wtab(hn + 1)
                    wi = head_wt[hn]
                    if hl == 1:
                        bg_round(2 * pi + 1)
                    attention_head(kind, chunks, QTm[hl], KT1, b_QTm[hl], b_KT1,
                                   G * 512 + h * 64, h, wtab[wi], b_wtab[wi])
            if bg_tab:
                bg_round(16)
                built["p"] = True
            pv_flush()
            cx.barrier()
            cx.dma("sp", wout_sb[:], w_out_b.rearrange("(c p) n -> p c n", p=128), [b_scr["w_out_b"]], [b_wout], b_wout)
            cx.dma("pool", wd_sb[:], w_down_b.rearrange("(c p) n -> p c n", p=128), [b_scr["w_down_b"]], [b_wd], b_wd)

            def mix_A(qd, tl):
                j = qd * 4 + tl
                st0 = stat_slot()
                sa_ = rms_sd(o_sb[:, j, 0:512], b_o[j], 512, st0)
                st1 = stat_slot()
                sb_ = rms_sd(o_sb[:, j, 512:1024], b_o[j], 512, st1)
                hi = nxt("hb", NHB)
                rms_apply(hb[hi][:, 0:512], b_hb[hi], o_sb[:, j, 0:512], b_o[j], sa_, st0, gb[:, 1, 0:512])
                rms_apply(hb[hi][:, 512:1024], b_hb[hi], o_sb[:, j, 512:1024], b_o[j], sb_, st1, gb[:, 1, 512:1024])
                return hi

            def mix_phase(qd):
                prev = None
                for tl in range(4):
                    hi = mix_A(qd, tl)
                    if prev is not None:
                        norm_tile_B(prev[0], mixT, b_mixT, prev[1] * 128)
                    prev = (hi, tl)
                norm_tile_B(prev[0], mixT, b_mixT, prev[1] * 128)

            def outproj_phase(qd):
                for tl in range(4):
                    j = qd * 4 + tl
                    xi = nxt("xt", NXT)
                    r0 = row0 + j * 128
                    cx.dma("sp", xt[xi][:], xin[r0:r0 + 128, :], [], [b_xt[xi]], b_xt[xi])
                    for hf in range(2):
                        mi = nxt("mm", NMM)
                        cx.op("pe", [(lambda kc=kc: pe.matmul(mm[mi][:, :], lhsT=mixT[:, kc, tl * 128:(tl + 1) * 128],
                                                              rhs=wout_sb[:, kc, hf * 512:(hf + 1) * 512],
                                                              start=(kc == 0), stop=(kc == 7))) for kc in range(8)],
                              [b_mixT, b_wout], [b_mm[mi]])
                        cx.op("dve", lambda: dve.tensor_tensor(out=x1[:, tl, hf * 512:(hf + 1) * 512],
                                                               in0=mm[mi][:, :], in1=xt[xi][:, hf * 512:(hf + 1) * 512],
                                                               op=ALU.add),
                              [b_mm[mi], b_xt[xi]], [b_x1[tl]])
                prev = None
                for tl in range(4):
                    hi = norm_tile_A(x1[:, tl, :], b_x1[tl], 2)
                    if prev is not None:
                        norm_tile_B(prev[0], hn2T, b_hn2T, prev[1] * 128)
                    prev = (hi, tl)
                norm_tile_B(prev[0], hn2T, b_hn2T, prev[1] * 128)

            def ffn_phase(qd, mix_next=False):
                mix_hi = {}
                for jf in range(NJ):
                    if mix_next:
                        if jf % 5 == 1 and jf // 5 < 4:
                            mix_hi[jf // 5] = mix_A(qd + 1, jf // 5)
                        if jf % 5 == 4 and jf // 5 < 4:
                            norm_tile_B(mix_hi[jf // 5], mixT, b_mixT, (jf // 5) * 128)
                    wi = nxt("wgu", 2)
                    load_w_chunk(wg_sb[wi], b_wg[wi], w_gate_b, b_scr["w_gate_b"], jf * 128, 128)
                    load_w_chunk(wu_sb[wi], b_wu[wi], w_up_b, b_scr["w_up_b"], jf * 128, 128)
                    mg = nxt("mm", NMM)
                    cx.op("pe", [(lambda kc=kc: pe.matmul(mm[mg][:, :], lhsT=wg_sb[wi][:, kc, :], rhs=hn2T[:, kc, :],
                                                          start=(kc == 0), stop=(kc == 7))) for kc in range(8)],
                          [b_wg[wi], b_hn2T], [b_mm[mg]])
                    mu = nxt("mm", NMM)
                    cx.op("pe", [(lambda kc=kc: pe.matmul(mm[mu][:, :], lhsT=wu_sb[wi][:, kc, :], rhs=hn2T[:, kc, :],
                                                          start=(kc == 0), stop=(kc == 7))) for kc in range(8)],
                          [b_wu[wi], b_hn2T], [b_mm[mu]])
                    si = nxt("sg", 2)
                    cx.op("act", lambda: act.activation(out=sg[si][:], in_=mm[mg][:, :], func=AF.Silu),
                          [b_mm[mg]], [b_sg[si]])
                    cx.op("dve", lambda: dve.tensor_tensor(out=hT[:, jf, :], in0=mm[mu][:, :], in1=sg[si][:], op=ALU.mult),
                          [b_mm[mu], b_sg[si]], [b_hT[jf]])

            def down_phase(qd):
                def final_norm(yi, j):
                    st0 = stat_slot()
                    sd = rms_sd(yt[yi][:], b_yt[yi], D, st0)
                    rms_apply(yt[yi][:], b_yt[yi], yt[yi][:], b_yt[yi], sd, st0, gb[:, 3, :])
                    r0 = row0 + j * 128
                    cx.dma("pool", yout[r0:r0 + 128, :], yt[yi][:], [b_yt[yi]], [], b_yt[yi])

                prev = None
                for tl in range(4):
                    j = qd * 4 + tl
                    yi = nxt("yt", 2)
                    for hf in range(2):
                        mi = nxt("mm", NMM)
                        cx.op("pe", [(lambda jf=jf: pe.matmul(mm[mi][:, :], lhsT=hT[:, jf, tl * 128:(tl + 1) * 128],
                                                              rhs=wd_sb[:, jf, hf * 512:(hf + 1) * 512],
                                                              start=(jf == 0), stop=(jf == NJ - 1))) for jf in range(NJ)],
                              b_hT + [b_wd], [b_mm[mi]])
                        cx.op("dve", lambda: dve.tensor_tensor(out=yt[yi][:, hf * 512:(hf + 1) * 512], in0=mm[mi][:, :],
                                                               in1=x1[:, tl, hf * 512:(hf + 1) * 512], op=ALU.add),
                              [b_mm[mi], b_x1[tl]], [b_yt[yi]])
                    if prev is not None:
                        final_norm(*prev)
                    prev = (yi, j)
                final_norm(*prev)

            mix_phase(0)
            for qd in range(4):
                outproj_phase(qd)
                ffn_phase(qd, mix_next=(qd + 1 < 4))
                down_phase(qd)
        cx.barrier()
    return nc


_DIL_TABLE = None


def make_in_maps(x_prompt, x_sample, w_in, rpb, g_attn, g_na, g_dil, w_out, g_ffn, w_gate, w_up, w_down, g_final,
                 n_cores=8, n_sample=4, do_prompt=True):
    global _DIL_TABLE
    if _DIL_TABLE is None:
        _DIL_TABLE = host_dil_table()
    f = lambda a: np.ascontiguousarray(np.asarray(a, dtype=np.float32))
    gvec = np.stack([f(g_attn)[0], np.concatenate([f(g_na)[0], f(g_dil)[0]]), f(g_ffn)[0], f(g_final)], axis=0)
    rpb0 = f(rpb)[0]
    bna_s = host_na_table(rpb0, "s", 0)
    bna_p = [host_na_table(rpb0, "p", 0), host_na_table(rpb0, "p", 1)]
    xs = f(x_sample)
    xp = f(x_prompt)
    maps = []
    for c in range(n_cores):
        rows = [xs[n_sample * c + i] for i in range(n_sample)]
        half = c % 2
        if do_prompt:
            s = c // 2
            rows.append(xp[s, half * UT:(half + 1) * UT])
            rows.append(xp[s, UT:UT + HALO] if half == 0 else xp[s, HALO:UT])
        g = np.zeros((128, 2), np.float32)
        g[:, half] = 1.0
        maps.append({
            "xin": np.ascontiguousarray(np.concatenate(rows, axis=0)),
            "w_in": f(w_in)[0], "w_out": f(w_out)[0], "w_gate": f(w_gate)[0], "w_up": f(w_up)[0],
            "w_down": f(w_down)[0], "gv": gvec, "bdil": _DIL_TABLE, "bna_s": bna_s, "bna_p": bna_p[half],
            "gates": g,
        })
    return maps


_NC_CACHE = {}


def kernel(x_prompt, x_sample, w_in, rpb, g_attn, g_na, g_dil, w_out, g_ffn, w_gate, w_up, w_down, g_final):
    if "nc" not in _NC_CACHE:
        _NC_CACHE["nc"] = build_program(4, True)
    nc = _NC_CACHE["nc"]
    maps = make_in_maps(x_prompt, x_sample, w_in, rpb, g_attn, g_na, g_dil, w_out, g_ffn, w_gate, w_up, w_down, g_final)
    res = run_bass_kernel_spmd(nc, maps, core_ids=list(range(8)))
    y_sample = np.empty((32, 2048, D), np.float32)
    y_prompt = np.empty((4, 4096, D), np.float32)
    for c in range(8):
        y = np.asarray(res.results[c]["yout"], dtype=np.float32)
        for i in range(4):
            y_sample[4 * c + i] = y[i * UT:(i + 1) * UT]
        y_prompt[c // 2, (c % 2) * UT:(c % 2 + 1) * UT] = y[4 * UT:5 * UT]
    return (y_prompt, y_sample)
```

```python
import contextlib
import numpy as np
import concourse.bass as bass
import concourse.mybir as mybir
from concourse.bass_utils import run_bass_kernel_spmd

F32 = mybir.dt.float32
BF16 = mybir.dt.bfloat16
AF = mybir.ActivationFunctionType
ALU = mybir.AluOpType

D = 1024
DFF = 2816
NJ = DFF // 128
UT = 2048
HALO = 1024
NEGB = -30000.0
EPS = 1e-6
N_DIL_T = 17
NPW = 7
NPE = 4
NHB = 3
NXT = 3
NA_INT = 5


def na_special_pairs(kind):
    if kind == "s":
        return [(0, k) for k in range(4)] + [(1, k) for k in range(4)] + \
               [(14, k) for k in range(12, 16)] + [(15, k) for k in range(12, 16)]
    return [(0, k) for k in (0, 1, 2, 3, 22, 23)] + [(1, k) for k in (0, 1, 2, 3, 23)] + \
           [(14, k) for k in (12, 13, 14, 15, 16)] + [(15, k) for k in (12, 13, 14, 15, 16, 17)]


def na_ntypes(kind):
    return NA_INT + len(na_special_pairs(kind))


def na_steps(kind):
    sp = na_special_pairs(kind)
    sp_idx = {p: NA_INT + i for i, p in enumerate(sp)}
    nslots = 16 if kind == "s" else 24
    chunks = []
    for j0 in range(0, 16, 4):
        steps = []
        js = list(range(j0, j0 + 4))
        for j in js:
            if j in (0, 1, 14, 15):
                for (jj, k) in sp:
                    if jj == j:
                        steps.append((k, j, j + 1, sp_idx[(j, k)], None))
        interior = [j for j in js if j not in (0, 1, 14, 15)]
        if interior:
            for k in range(nslots):
                v = [j for j in interior if abs(k - j) <= 2 and k < 16]
                if v:
                    ja, jb = v[0], v[-1] + 1
                    steps.append((k, ja, jb, 2 - (k - ja), None))
        chunks.append((j0, steps))
    return chunks


def dil_steps(kind):
    chunks = []
    for j0 in range(0, 16, 4):
        steps = []
        js = list(range(j0, j0 + 4))
        for k in range(16):
            v = [j for j in js if abs(k - j) <= 8]
            if v:
                ja, jb = v[0], v[-1] + 1
                steps.append((k, ja, jb, 8 - (k - ja), None))
        if kind == "p":
            for s in range(16, 24):
                v = [j for j in js if j >= 8 and s <= j + 8]
                if v:
                    ja, jb = v[0], v[-1] + 1
                    steps.append((s, ja, jb, 8 - s + ja, 0))
                v = [j for j in js if j <= 7 and s >= j + 16]
                if v:
                    ja, jb = v[0], v[-1] + 1
                    steps.append((s, ja, jb, 32 - s + ja, 1))
        chunks.append((j0, steps))
    return chunks


def host_dil_table():
    slopes = np.array([2.0 ** (-(i + 1)) for i in range(8)], dtype=np.float64)
    k = np.arange(128)[:, None, None]
    i = np.arange(N_DIL_T)[None, :, None]
    q = np.arange(128)[None, None, :]
    delta = 128 * (8 - i) + k - q
    ad = np.abs(delta)
    mult = (ad <= 64).astype(np.float64) + ((delta % 4 == 0) & (ad <= 256)) + ((delta % 16 == 0) & (ad <= 1024))
    out = np.empty((8, 128, N_DIL_T, 128), dtype=np.float32)
    for h in range(8):
        out[h] = (mult * np.exp(-slopes[h] * ad)).astype(np.float32)
    return out.reshape(8, 128, N_DIL_T * 128)


def _na_tile(rpb, R, qrows, krows):
    out = np.full((8, 128, 128), NEGB, dtype=np.float32)
    kc = np.arange(64)[:, None]
    qc = np.arange(64)[None, :]
    ws = np.clip(qc - 8, 0, 48)
    ok = (kc >= ws) & (kc < ws + 16)
    rel = np.clip(kc - qc + 15, 0, 30)
    for ki, kr in enumerate(krows):
        for qi, qr in enumerate(qrows):
            if kr is None or kr < 0 or kr >= R:
                continue
            rs = int(np.clip(qr - 4, 0, R - 8))
            if not (rs <= kr < rs + 8):
                continue
            roff = kr - qr + 7
            g = rpb[:, roff][:, rel]
            out[:, ki * 64:(ki + 1) * 64, qi * 64:(qi + 1) * 64] = np.where(ok[None], g, np.float32(NEGB))
    return out


def host_na_table(rpb, kind, half):
    nt = na_ntypes(kind)
    out = np.empty((8, 128, nt, 128), dtype=np.float32)
    for i in range(NA_INT):
        dlt = 2 - i
        out[:, :, i, :] = _na_tile(rpb, 64, (16, 17), (16 + 2 * dlt, 17 + 2 * dlt))
    for n, (j, slot) in enumerate(na_special_pairs(kind)):
        if kind == "s":
            R = 32
            qrows = (2 * j, 2 * j + 1)
            krows = (2 * slot, 2 * slot + 1)
        else:
            R = 64
            if half == 0:
                qrows = (2 * j, 2 * j + 1)
                krows = (2 * slot, 2 * slot + 1)
            else:
                qrows = (32 + 2 * j, 33 + 2 * j)
                if slot < 16:
                    krows = (32 + 2 * slot, 33 + 2 * slot)
                else:
                    g = slot - 8
                    krows = (2 * g, 2 * g + 1)
        out[:, :, NA_INT + n, :] = _na_tile(rpb, R, qrows, krows)
    return out.reshape(8, 128, nt * 128)


class Buf:
    __slots__ = ("name", "w", "r", "dsem", "dcnt", "dma")

    def __init__(self, name):
        self.name = name
        self.w = None
        self.r = {}
        self.dsem = {}
        self.dcnt = {}
        self.dma = False


class Ctx:
    def __init__(self, nc, es):
        self.nc = nc
        self.es = es
        self.eng = {"pe": nc.tensor, "act": nc.scalar, "dve": nc.vector, "pool": nc.gpsimd, "sp": nc.sync}
        self.sem = {e: es.enter_context(nc.semaphore("s_" + e)) for e in self.eng}
        self.cnt = {e: 0 for e in self.eng}
        self.seen = {e: {} for e in self.eng}
        self.semname = {}
        self.dma_toks = []
        self.nsem = 0
        self.defer = None

    def buf(self, name, dma=False):
        b = Buf(name)
        b.dma = dma
        return b

    def _wait(self, e, tok, raw):
        if tok is None:
            return
        te, sem, val = tok
        if te == e:
            if e == "pe" or e == "sp" or (not raw and e != "pool"):
                return
        key = id(sem)
        if self.seen[e].get(key, 0) >= val:
            return
        self.seen[e][key] = val
        if self.defer is not None:
            self.defer.append((sem, val))
        else:
            self.eng[e].wait_ge(sem, val)

    def _sync(self, e, reads, writes):
        for b in reads:
            self._wait(e, b.w, True)
        for b in writes:
            self._wait(e, b.w, False)
            for t in b.r.values():
                self._wait(e, t, False)

    def _mark(self, tok, reads, writes):
        for b in reads:
            b.r[id(tok[1])] = tok
        for b in writes:
            b.w = tok
            b.r = {}

    def op(self, e, fns, reads=(), writes=()):
        self.defer = []
        self._sync(e, reads, writes)
        pend, self.defer = self.defer, None
        for (sem_, val_) in pend[:-1]:
            self.eng[e].wait_ge(sem_, val_)
        if not isinstance(fns, (list, tuple)):
            fns = [fns]
        ins = None
        for i_, f in enumerate(fns):
            ins = f()
            if i_ == 0 and pend:
                ins._wait_ge(pend[-1][0], pend[-1][1])
        self.cnt[e] += 1
        ins.then_inc(self.sem[e], 1)
        tok = (e, self.sem[e], self.cnt[e])
        self._mark(tok, reads, writes)
        return tok

    def dma(self, q, out_ap, in_ap, reads, writes, sembuf, waw=True, track=True):
        if waw:
            self._sync(q, reads, writes)
        else:
            self._sync(q, reads, [])
        cls = "sw" if q == "pool" else "hw"
        if cls not in sembuf.dsem:
            sembuf.dsem[cls] = self.es.enter_context(self.nc.semaphore("d%s_%s_%d" % (cls, sembuf.name, self.nsem)))
            sembuf.dcnt[cls] = 0
            self.nsem += 1
        sem = sembuf.dsem[cls]
        self.eng[q].dma_start(out=out_ap, in_=in_ap).then_inc(sem, 16)
        sembuf.dcnt[cls] += 16
        tok = ("dma", sem, sembuf.dcnt[cls])
        self._mark(tok, reads, writes)
        if track:
            self.dma_toks.append(tok)
        return tok

    def barrier(self):
        toks = [(e, self.sem[e], self.cnt[e]) for e in self.eng if self.cnt[e] > 0]
        last = {}
        for t in self.dma_toks:
            k = id(t[1])
            if k not in last or last[k][2] < t[2]:
                last[k] = t
        toks += list(last.values())
        self.dma_toks = list(last.values())
        for e in self.eng:
            for t in toks:
                if t[0] == e:
                    continue
                key = id(t[1])
                if self.seen[e].get(key, 0) >= t[2]:
                    continue
                self.seen[e][key] = t[2]
                self.eng[e].wait_ge(t[1], t[2])


def build_program(n_sample=4, do_prompt=True, debug=False):
    nc = bass.Bass("TRN2", target_bir_lowering=False)
    units = ["s"] * n_sample + (["p"] if do_prompt else [])
    nunits = len(units)
    n_in_rows = nunits * UT + (HALO if do_prompt else 0)

    def din(name, shape, dt=F32):
        return nc.dram_tensor(name, list(shape), dt, kind="ExternalInput").ap()

    def dscr(name, shape, dt=BF16):
        return nc.dram_tensor(name, list(shape), dt, kind="Internal").ap()

    xin = din("xin", [n_in_rows, D])
    w_in = din("w_in", [D, 3 * D])
    w_out = din("w_out", [D, D])
    w_gate = din("w_gate", [D, DFF])
    w_up = din("w_up", [D, DFF])
    w_down = din("w_down", [DFF, D])
    gv = din("gv", [4, D])
    bdil = din("bdil", [8, 128, N_DIL_T * 128])
    nts, ntp = na_ntypes("s"), na_ntypes("p")
    bna_s = din("bna_s", [8, 128, nts * 128])
    bna_p = din("bna_p", [8, 128, ntp * 128])
    gates_in = din("gates", [128, 2])
    yout = nc.dram_tensor("yout", [nunits * UT, D], F32, kind="ExternalOutput").ap()

    w_in_b = dscr("w_in_b", [D, 3 * D])
    w_out_b = dscr("w_out_b", [D, D])
    w_gate_b = dscr("w_gate_b", [D, DFF])
    w_up_b = dscr("w_up_b", [D, DFF])
    w_down_b = dscr("w_down_b", [DFF, D])
    wdil = dscr("wdil", [8, 128, N_DIL_T * 128])
    wna_s = dscr("wna_s", [8, 128, nts * 128])
    wna_p = dscr("wna_p", [8, 128, ntp * 128])
    NTMAX = max(nts, ntp)

    es = contextlib.ExitStack()
    with es:
        cx = Ctx(nc, es)
        pe, act, dve, pool, sp = nc.tensor, nc.scalar, nc.vector, nc.gpsimd, nc.sync

        def sb(name, shape, dt):
            return es.enter_context(nc.sbuf_tensor(name, list(shape), dt))

        def ps(name, shape, dt):
            return es.enter_context(nc.psum_tensor(name, list(shape), dt))

        gb = sb("gb", [128, 4, D], F32)
        ident = sb("ident", [128, 128], BF16)
        identf = sb("identf", [128, 128], F32)
        gates = sb("gates_sb", [128, 2], F32)
        xt = [sb("xt%d" % i, [128, D], F32) for i in range(NXT)]
        hb = [sb("hb%d" % i, [128, D], BF16) for i in range(NHB)]
        o_sb = sb("o_sb", [128, 16, D], BF16)
        junk = sb("junk", [128, D], BF16)
        yt = [sb("yt%d" % i, [128, D], F32) for i in range(2)]
        stat = sb("stat", [128, 64], F32)
        eps_t = sb("eps_t", [128, 1], F32)
        rden = [sb("rden%d" % i, [128, 4], F32) for i in range(2)]
        REG_ELEMS = 65536
        reg = sb("region", [128, REG_ELEMS], BF16)
        regpos = [0]

        def carve(shape_free, dt):
            n = int(np.prod(shape_free))
            ne = n * (2 if dt == F32 else 1)
            off = regpos[0]
            off = (off + 1) // 2 * 2
            assert off + ne <= REG_ELEMS, (off, ne)
            regpos[0] = off + ne
            ap = reg[:, off:off + ne]
            if dt == F32:
                ap = ap.bitcast(F32)
            if len(shape_free) == 2:
                ap = ap.rearrange("p (a b) -> p a b", a=shape_free[0])
            elif len(shape_free) == 3:
                ap = ap.rearrange("p (a b c) -> p a b c", a=shape_free[0], b=shape_free[1])
            return ap

        regpos[0] = 0
        hnT = carve([8, UT + HALO], BF16)
        Vflat = carve([24 * 8, 65], BF16)
        Vsb = Vflat.rearrange("p (a b) c -> p a b c", a=24)
        qtm_off = (regpos[0] + 1) // 2 * 2
        QTm = [carve([UT], BF16) for _ in range(2)]
        KT1 = carve([UT + HALO], BF16)
        kt_end = regpos[0]
        wv_sb = carve([8, 512], BF16)
        wq_sb = [carve([8, 128], BF16) for _ in range(2)]
        wk_sb = [carve([8, 128], BF16) for _ in range(2)]
        wtab = [carve([NTMAX * 128], BF16) for _ in range(2)]
        pexp = [carve([512], BF16) for _ in range(NPE)]
        pw = [carve([512], BF16) for _ in range(NPW)]
        endA = regpos[0]
        regpos[0] = 0
        hnT_f32 = carve([8, (UT + HALO) // 2], F32)
        regpos[0] = qtm_off
        btmp = carve([NTMAX * 128], F32)
        assert regpos[0] <= kt_end
        regpos[0] = 0
        wout_sb = carve([8, D], BF16)
        wd_sb = carve([NJ, D], BF16)
        x1 = carve([4, D], F32)
        mixT = carve([8, 512], BF16)
        hn2T = carve([8, 512], BF16)
        hT = carve([NJ, 512], BF16)
        wg_sb = [carve([8, 128], BF16) for _ in range(2)]
        wu_sb = [carve([8, 128], BF16) for _ in range(2)]
        sg = [carve([512], F32) for _ in range(2)]
        endB = regpos[0]

        NMM = 6
        mm = [ps("mm%d" % i, [128, 512], F32) for i in range(NMM)]
        tp = [m[:, :].bitcast(BF16).rearrange("p (c n) -> p c n", c=8) for m in mm]
        pv = [ps("pv%d" % i, [128, 4, 128], F32) for i in range(2)]

        B = cx.buf
        b_gb = B("gb", dma=True)
        b_ident = B("ident")
        b_gates = B("gates", dma=True)
        b_xt = [B("xt%d" % i, dma=True) for i in range(NXT)]
        b_hb = [B("hb%d" % i) for i in range(NHB)]
        b_o = [B("o%d" % i) for i in range(16)]
        b_junk = B("junk")
        b_yt = [B("yt%d" % i, dma=True) for i in range(2)]
        b_stats = [B("stat%d" % i) for i in range(8)]
        b_rden = [B("rden%d" % i) for i in range(2)]
        b_hnT = [B("hnT%d" % i) for i in range(6)]
        b_V = [B("V%d" % i) for i in range(24)]
        b_QTm = [B("QTm%d" % i) for i in range(2)]
        b_KT1 = B("KT1")
        b_wv = B("wv", dma=True)
        b_wq = [B("wq%d" % i, dma=True) for i in range(2)]
        b_wk = [B("wk%d" % i, dma=True) for i in range(2)]
        b_wtab = [B("wtab%d" % i, dma=True) for i in range(2)]
        b_pexp = [B("pexp%d" % i) for i in range(NPE)]
        b_pw = [B("pw%d" % i) for i in range(NPW)]
        b_tabld = B("tabld", dma=True)
        b_eps = B("eps")
        b_wout = B("wout", dma=True)
        b_wd = B("wd", dma=True)
        b_x1 = [B("x1_%d" % i) for i in range(4)]
        b_mixT = B("mixT")
        b_hn2T = B("hn2T")
        b_hT = [B("hT%d" % i) for i in range(NJ)]
        b_wg = [B("wg%d" % i, dma=True) for i in range(2)]
        b_wu = [B("wu%d" % i, dma=True) for i in range(2)]
        b_sg = [B("sg%d" % i) for i in range(2)]
        b_mm = [B("mm%d" % i) for i in range(NMM)]
        b_tp = b_mm
        b_pv = [B("pv%d" % i) for i in range(2)]
        b_scr = {n: B(n, dma=True) for n in ("w_out_b", "w_gate_b", "w_up_b", "w_down_b")}
        b_win = [B("w_in_b%d" % i, dma=True) for i in range(6)]
        gate_t = sb("gate_t", [128, 2], F32)
        b_gate_t = B("gate_t")
        b_wdil = [B("wdil%d" % h, dma=True) for h in range(8)]
        b_wnas = [B("wnas%d" % h, dma=True) for h in range(8)]
        b_wnap = [B("wnap%d" % h, dma=True) for h in range(8)]

        rr = {"mm": 0, "tp": 0, "xt": 0, "hb": 0, "yt": 0, "pe": 0, "pw": 0, "pv": 0, "rden": 0, "wt": 0,
              "qk": 0, "wgu": 0, "sg": 0, "st": 0}

        def nxt(key, n):
            i = rr[key]
            rr[key] = (i + 1) % n
            return i

        def stat_slot():
            i = nxt("st", 8)
            return i * 8

        def cast_weight(name, src, dst):
            nr = src.shape[0]
            for r0 in range(0, nr, 256):
                r1 = min(nr, r0 + 256)
                cx.dma("pool", dst[r0:r1, :], src[r0:r1, :], [], [b_scr[name]], b_scr[name], waw=False, track=False)

        def cast_win_block(i):
            for r0 in range(0, D, 256):
                cx.dma("pool", w_in_b[r0:r0 + 256, i * 512:(i + 1) * 512], w_in[r0:r0 + 256, i * 512:(i + 1) * 512],
                       [], [b_win[i]], b_win[i], waw=False, track=False)

        def pool_gate(bufs):
            cx.op("pool", lambda: pool.memset(gate_t[:], 0.0), bufs, [b_gate_t])

        for i in (2, 0, 1):
            cast_win_block(i)
        for i in range(4):
            cx.dma("sp", gb[:, i, :], gv[i:i + 1, :].partition_broadcast(128), [], [b_gb], b_gb, waw=False)
        cx.dma("sp", gates[:], gates_in[:, :], [], [b_gates], b_gates)
        cx.op("pool", lambda: pool.memset(identf[:], 1.0), [], [b_ident])
        cx.op("pool", lambda: pool.affine_select(out=identf[:], in_=identf[:], pattern=[[-1, 128]],
                                                compare_op=ALU.is_equal, fill=0.0, base=0, channel_multiplier=1),
              [b_ident], [b_ident])
        cx.op("dve", lambda: dve.tensor_copy(ident[:], identf[:]), [b_ident], [b_ident])
        cx.op("dve", lambda: dve.memset(eps_t[:], EPS), [], [b_eps])

        b_alias = [b_QTm[0], b_QTm[1], b_KT1]

        def build_table_head(src, dst, bl, ncols, h):
            cx.dma("sp", btmp[:, 0:ncols], src[h], [], b_alias, b_tabld)
            wb = wtab[h % 2]
            cx.op("act", (lambda wb=wb: act.activation(out=wb[:, 0:ncols], in_=btmp[:, 0:ncols], func=AF.Exp)),
                  b_alias, [b_wtab[h % 2]])
            cx.dma("sp", dst[h], wb[:, 0:ncols], [b_wtab[h % 2]], [bl[h]], bl[h])

        built = {"s": False, "p": False, "casts": False}
        cast_jobs = []
        cast_gate = [None]
        def rms_sd(src_ap, b_src, ncols, st0):
            b_stat = b_stats[st0 // 8]
            cx.op("act", lambda: act.activation(out=junk[:, 0:ncols], in_=src_ap, func=AF.Square,
                                                accum_out=stat[:, st0:st0 + 1]),
                  [b_src], [b_junk, b_stat])
            cx.op("act", lambda: act.activation(out=stat[:, st0 + 1:st0 + 2], in_=stat[:, st0:st0 + 1],
                                                func=AF.Sqrt, scale=1.0 / ncols, bias=eps_t[:, 0:1]),
                  [b_stat, b_eps], [b_stat])
            return stat[:, st0 + 1:st0 + 2]

        def rms_apply(out_ap, b_out, src_ap, b_src, sd, st0, g_ap):
            b_stat = b_stats[st0 // 8]
            cx.op("dve", lambda: dve.reciprocal(out=stat[:, st0 + 2:st0 + 3], in_=sd), [b_stat], [b_stat])
            cx.op("dve", lambda: dve.scalar_tensor_tensor(out=out_ap, in0=src_ap, scalar=stat[:, st0 + 2:st0 + 3], in1=g_ap,
                                                          op0=ALU.mult, op1=ALU.mult),
                  [b_src, b_stat, b_gb], [b_out])

        def transpose_to(src_bf, b_src, dstT, b_dst, col0):
            ti = nxt("mm", NMM)
            cx.op("pe", [(lambda c=c: pe.transpose(tp[ti][:, c, :], src_bf[:, c * 128:(c + 1) * 128], ident[:]))
                         for c in range(8)],
                  [b_src, b_ident], [b_tp[ti]])
            cx.op("act", lambda: act.copy(out=dstT[:, :, col0:col0 + 128], in_=tp[ti]),
                  [b_tp[ti]], [b_dst])

        def norm_tile_A(src_ap, b_src, gidx):
            st0 = stat_slot()
            sd = rms_sd(src_ap, b_src, D, st0)
            hi = nxt("hb", NHB)
            rms_apply(hb[hi][:], b_hb[hi], src_ap, b_src, sd, st0, gb[:, gidx, :])
            return hi

        def norm_tile_B(hi, dstT, b_dst, col0):
            transpose_to(hb[hi], b_hb[hi], dstT, b_dst, col0)

        def load_w_chunk(dst, b_dst, src_scr, b_src, col0, ncol, q="sp"):
            if src_scr is w_in_b:
                b_src = b_win[col0 // 512]
            cx.dma(q, dst[:, :, 0:ncol], src_scr[:, col0:col0 + ncol].rearrange("(c p) n -> p c n", p=128),
                   [b_src], [b_dst], b_dst)

        def proj_T(w_sb, b_w, srcT, b_srcs, col0, n, dst_ap, b_dst):
            mi = nxt("mm", NMM)
            cx.op("pe", [(lambda kc=kc: pe.matmul(mm[mi][:, 0:n], lhsT=w_sb[:, kc, :], rhs=srcT[:, kc, col0:col0 + n],
                                                  start=(kc == 0), stop=(kc == 7))) for kc in range(8)],
                  [b_w] + b_srcs, [b_mm[mi]])
            cx.op("dve", lambda: dve.tensor_copy(dst_ap, mm[mi][:, 0:n]), [b_mm[mi]], [b_dst])

        PV_DEPTH = 4
        pvq = []

        def pv_emit(ent):
            (k, ja, jb, pwi, pvi, j0, hv, hcol, first, last) = ent
            fns = []
            for n_, j in enumerate(range(ja, jb)):
                st_flag = first and n_ == 0
                fns.append(lambda n_=n_, j=j, st_flag=st_flag: pe.matmul(
                    pv[pvi][:, j - j0, 0:65], lhsT=pw[pwi][:, n_ * 128:(n_ + 1) * 128],
                    rhs=Vsb[:, k, hv, :], start=st_flag, stop=False, skip_group_check=True))
            cx.op("pe", fns, [b_pw[pwi], b_V[k]], [b_pv[pvi]])
            if last:
                ri = nxt("rden", 2)
                cx.op("dve", lambda: dve.reciprocal(out=rden[ri][:], in_=pv[pvi][:, :, 64]),
                      [b_pv[pvi]], [b_rden[ri]])
                cx.op("dve", lambda: dve.tensor_tensor(out=o_sb[:, j0:j0 + 4, hcol:hcol + 64], in0=pv[pvi][:, :, 0:64],
                                                       in1=rden[ri][:].unsqueeze(2).to_broadcast([128, 4, 64]),
                                                       op=ALU.mult),
                      [b_pv[pvi], b_rden[ri]], [b_o[j] for j in range(j0, j0 + 4)])

        def pv_flush(keep=0):
            while len(pvq) > keep:
                pv_emit(pvq.pop(0))

        def attention_head(kind, chunks, qt_ap, kt_ap, b_qt, b_kt, hcol, hv, wt_ap, b_wt):
            for (j0, steps) in chunks:
                pvi = nxt("pv", 2)
                ns = len(steps)
                for si_, (k, ja, jb, ti, gate) in enumerate(steps):
                    n = (jb - ja) * 128
                    mi = nxt("mm", NMM)
                    cx.op("pe", lambda: pe.matmul(mm[mi][:, 0:n], lhsT=kt_ap[:, k * 128:(k + 1) * 128],
                                                  rhs=qt_ap[:, ja * 128:jb * 128], start=True, stop=True),
                          [b_kt, b_qt], [b_mm[mi]])
                    pv_flush(PV_DEPTH)
                    pei = nxt("pe", NPE)
                    cx.op("act", lambda: act.activation(out=pexp[pei][:, 0:n], in_=mm[mi][:, 0:n], func=AF.Exp,
                                                        scale=0.125),
                          [b_mm[mi]], [b_pexp[pei]])
                    pwi = nxt("pw", NPW)
                    if gate is None:
                        cx.op("dve", lambda: dve.tensor_tensor(out=pw[pwi][:, 0:n], in0=pexp[pei][:, 0:n],
                                                               in1=wt_ap[:, ti * 128:ti * 128 + n], op=ALU.mult),
                              [b_pexp[pei], b_wt], [b_pw[pwi]])
                    else:
                        cx.op("dve", lambda: dve.scalar_tensor_tensor(out=pw[pwi][:, 0:n], in0=pexp[pei][:, 0:n],
                                                                      scalar=gates[:, gate:gate + 1],
                                                                      in1=wt_ap[:, ti * 128:ti * 128 + n],
                                                                      op0=ALU.mult, op1=ALU.mult),
                              [b_pexp[pei], b_wt, b_gates], [b_pw[pwi]])
                    pvq.append((k, ja, jb, pwi, pvi, j0, hv, hcol, si_ == 0, si_ == ns - 1))

        steps_cache = {("na", "s"): na_steps("s"), ("na", "p"): na_steps("p"),
                       ("dil", "s"): dil_steps("s"), ("dil", "p"): dil_steps("p")}
        for u, kind in enumerate(units):
            row0 = u * UT
            nkt = 24 if kind == "p" else 16
            ntok = nkt * 128

            def xrows(t):
                if t < 16:
                    return row0 + t * 128
                return nunits * UT + (t - 16) * 128

            def vproj(t):
                mi = nxt("mm", NMM)
                cx.op("pe", [(lambda kc=kc: pe.matmul(mm[mi][:, :], lhsT=hnT[:, kc, t * 128:(t + 1) * 128],
                                                      rhs=wv_sb[:, kc, :], start=(kc == 0), stop=(kc == 7)))
                             for kc in range(8)],
                      [b_wv, b_hnT[t // 4]], [b_mm[mi]])
                cx.op("dve", lambda: dve.tensor_copy(Vsb[:, t, :, 0:64],
                                                     mm[mi][:, :].rearrange("p (h d) -> p h d", h=8)),
                      [b_mm[mi]], [b_V[t]])

            if u > 0:
                cx.barrier()
            cx.op("dve", lambda: dve.memset(Vflat[:, :, 64:65], 1.0), [], b_V)
            wq_ = "pool" if u == 0 else "sp"
            load_w_chunk(wv_sb, b_wv, w_in_b, None, 1024, 512, q=wq_)
            qi0 = nxt("qk", 2)
            load_w_chunk(wq_sb[qi0], b_wq[qi0], w_in_b, None, 0, 128, q=wq_)
            load_w_chunk(wk_sb[qi0], b_wk[qi0], w_in_b, None, 512, 128, q=wq_)
            early_qk = built[kind]

            def zero_q_halves():
                cx.op("pool", lambda: pool.memset(QTm[0][64:128, :], 0.0), [], [b_QTm[0]])
                cx.op("pool", lambda: pool.memset(QTm[1][0:64, :], 0.0), [], [b_QTm[1]])

            def qproj(qi, tq):
                mi = nxt("mm", NMM)
                cx.op("pe", [(lambda kc=kc: pe.matmul(mm[mi][:, :], lhsT=wq_sb[qi][:, kc, :],
                                                      rhs=hnT[:, kc, tq * 512:(tq + 1) * 512],
                                                      start=(kc == 0), stop=(kc == 7))) for kc in range(8)],
                      [b_wq[qi], b_hnT[tq]], [b_mm[mi]])
                cx.op("dve", lambda: dve.tensor_copy(QTm[0][0:64, tq * 512:(tq + 1) * 512], mm[mi][0:64, :]),
                      [b_mm[mi]], [b_QTm[0]])
                cx.op("dve", lambda: dve.tensor_copy(QTm[1][64:128, tq * 512:(tq + 1) * 512], mm[mi][64:128, :]),
                      [b_mm[mi]], [b_QTm[1]])

            def kproj(qi, tk):
                proj_T(wk_sb[qi], b_wk[qi], hnT, [b_hnT[tk]], tk * 512, 512, KT1[:, tk * 512:(tk + 1) * 512], b_KT1)

            if early_qk:
                zero_q_halves()
            def after_tile(t):
                norm_tile_B(prev[0], hnT, b_hnT[t // 4], t * 128)
                vproj(t)
                if early_qk and t % 4 == 3:
                    if t // 4 < 4:
                        qproj(qi0, t // 4)
                    kproj(qi0, t // 4)
                if not built[kind] and t % 2 == 1 and t < 16:
                    if kind == "s":
                        build_table_head(bna_s, wna_s, b_wnas, nts * 128, t // 2)
                    else:
                        build_table_head(bna_p, wna_p, b_wnap, ntp * 128, t // 2)

            prev = None
            for t in range(nkt):
                xi = nxt("xt", NXT)
                r0 = xrows(t)
                cx.dma("sp", xt[xi][:], xin[r0:r0 + 128, :], [], [b_xt[xi]], b_xt[xi])
                hi = norm_tile_A(xt[xi][:], b_xt[xi], 0)
                if prev is not None:
                    after_tile(prev[1])
                prev = (hi, t)
            after_tile(prev[1])
            if not built[kind]:
                last_tab = b_wnas[7] if kind == "s" else b_wnap[7]
                built[kind] = True
                if not built["casts"]:
                    def _rows(name, src, dst, r0, r1):
                        def f():
                            for a in range(r0, r1, 256):
                                b_ = min(r1, a + 256)
                                cx.dma("pool", dst[a:b_, :], src[a:b_, :], [], [b_scr[name]], b_scr[name],
                                       waw=False, track=False)
                        return f

                    def _dil(h0):
                        def f():
                            for h in range(h0, h0 + 4):
                                cx.dma("pool", wdil[h], bdil[h], [], [b_wdil[h]], b_wdil[h], track=False)
                        return f

                    cast_jobs.extend([
                        (lambda: [cast_win_block(i) for i in (5, 3, 4)]),
                        _dil(0), _dil(4),
                        _rows("w_out_b", w_out, w_out_b, 0, D),
                        _rows("w_gate_b", w_gate, w_gate_b, 0, 512), _rows("w_gate_b", w_gate, w_gate_b, 512, D),
                        _rows("w_up_b", w_up, w_up_b, 0, 512), _rows("w_up_b", w_up, w_up_b, 512, D),
                        _rows("w_down_b", w_down, w_down_b, 0, 1408), _rows("w_down_b", w_down, w_down_b, 1408, DFF),
                    ])
                    cast_gate[0] = last_tab
                    built["casts"] = True
            if not early_qk:
                zero_q_halves()
            def release_casts(njobs, gate_bufs):
                if cast_jobs:
                    pool_gate(gate_bufs)
                    for _ in range(njobs):
                        if cast_jobs:
                            cast_jobs.pop(0)()

            if cast_jobs:
                release_casts(1, [cast_gate[0]])

            pairs = [(G, c) for G in range(2) for c in range(4)]
            pair_qi = {0: qi0}

            def load_pair(pi):
                G, c = pairs[pi]
                qi = nxt("qk", 2)
                load_w_chunk(wq_sb[qi], b_wq[qi], w_in_b, None, 1536 * G + c * 128, 128)
                load_w_chunk(wk_sb[qi], b_wk[qi], w_in_b, None, 1536 * G + 512 + c * 128, 128)
                pair_qi[pi] = qi

            head_wt = {}

            def load_wtab(hn):
                G, h = hn // 8, hn % 8
                wi = nxt("wt", 2)
                if G == 0:
                    src, bsrc, ncols = ((wna_s, b_wnas, nts * 128) if kind == "s" else (wna_p, b_wnap, ntp * 128))
                else:
                    src, bsrc, ncols = wdil, b_wdil, N_DIL_T * 128
                cx.dma("sp", wtab[wi][:, 0:ncols], src[h], [bsrc[h]], [b_wtab[wi]], b_wtab[wi])
                head_wt[hn] = wi

            bg_tab = (kind == "s" and not built["p"] and u + 1 < nunits and units[u + 1] == "p")
            PC = ntp * 128 // 16
            stage_f = hnT_f32[:, :, UT // 2:UT // 2 + PC]
            stage_b = hnT[:, :, UT + 2 * PC:UT + 3 * PC]
            b_stage = [b_hnT[4], b_hnT[5]]

            def bg_load(r):
                h, hf_ = r // 2, r % 2
                cx.dma("sp", stage_f, bna_p[h].rearrange("p (a b) -> p a b", a=16)[:, hf_ * 8:(hf_ + 1) * 8, :],
                       [], b_stage, b_tabld)

            def bg_exp_store(r):
                h, hf_ = r // 2, r % 2
                cx.op("act", lambda: act.activation(out=stage_b, in_=stage_f, func=AF.Exp), b_stage, b_stage)
                cx.dma("sp", wna_p[h].rearrange("p (a b) -> p a b", a=16)[:, hf_ * 8:(hf_ + 1) * 8, :], stage_b,
                       b_stage, [b_wnap[h]], b_wnap[h], waw=False)

            def bg_round(r):
                if bg_tab:
                    if r >= 1:
                        bg_exp_store(r - 1)
                    if r < 16:
                        bg_load(r)

            load_wtab(0)
            for pi, (G, c) in enumerate(pairs):
                bg_round(2 * pi)
                cb = 1536 * G
                gname = "na" if G == 0 else "dil"
                chunks = steps_cache[(gname, kind)]
                if G == 1 and c == 0:
                    pv_flush()
                    load_w_chunk(wv_sb, b_wv, w_in_b, None, cb + 1024, 512)
                    for t in range(nkt):
                        vproj(t)
                qi = pair_qi[pi]
                if not (pi == 0 and early_qk):
                    for tq in range(4):
                        qproj(qi, tq)
                    for tk in range(ntok // 512):
                        kproj(qi, tk)
                if pi >= 1:
                    release_casts(1 if pi < 6 else 2, [b_KT1])
                if pi + 1 < len(pairs):
                    load_pair(pi + 1)
                for hl in range(2):
                    hn = 2 * pi + hl
                    h = hn % 8
                    if hn + 1 < 16:
                        load_wtab(hn + 1)
                    wi = head_wt[hn]
                    if hl == 1:
                        bg_round(2 * pi + 1)
                    attention_head(kind, chunks, QTm[hl], KT1, b_QTm[hl], b_KT1,
                                   G * 512 + h * 64, h, wtab[wi], b_wtab[wi])
            if bg_tab:
                bg_round(16)
                built["p"] = True
            pv_flush()
            cx.barrier()
            cx.dma("sp", wout_sb[:], w_out_b.rearrange("(c p) n -> p c n", p=128), [b_scr["w_out_b"]], [b_wout], b_wout)
            cx.dma("pool", wd_sb[:], w_down_b.rearrange("(c p) n -> p c n", p=128), [b_scr["w_down_b"]], [b_wd], b_wd)

            def mix_A(qd, tl):
                j = qd * 4 + tl
                st0 = stat_slot()
                sa_ = rms_sd(o_sb[:, j, 0:512], b_o[j], 512, st0)
                st1 = stat_slot()
                sb_ = rms_sd(o_sb[:, j, 512:1024], b_o[j], 512, st1)
                hi = nxt("hb", NHB)
                rms_apply(hb[hi][:, 0:512], b_hb[hi], o_sb[:, j, 0:512], b_o[j], sa_, st0, gb[:, 1, 0:512])
                rms_apply(hb[hi][:, 512:1024], b_hb[hi], o_sb[:, j, 512:1024], b_o[j], sb_, st1, gb[:, 1, 512:1024])
                return hi

            def mix_phase(qd):
                prev = None
                for tl in range(4):
                    hi = mix_A(qd, tl)
                    if prev is not None:
                        norm_tile_B(prev[0], mixT, b_mixT, prev[1] * 128)
                    prev = (hi, tl)
                norm_tile_B(prev[0], mixT, b_mixT, prev[1] * 128)

            def outproj_phase(qd):
                for tl in range(4):
                    j = qd * 4 + tl
                    xi = nxt("xt", NXT)
                    r0 = row0 + j * 128
                    cx.dma("sp", xt[xi][:], xin[r0:r0 + 128, :], [], [b_xt[xi]], b_xt[xi])
                    for hf in range(2):
                        mi = nxt("mm", NMM)
                        cx.op("pe", [(lambda kc=kc: pe.matmul(mm[mi][:, :], lhsT=mixT[:, kc, tl * 128:(tl + 1) * 128],
                                                              rhs=wout_sb[:, kc, hf * 512:(hf + 1) * 512],
                                                              start=(kc == 0), stop=(kc == 7))) for kc in range(8)],
                              [b_mixT, b_wout], [b_mm[mi]])
                        cx.op("dve", lambda: dve.tensor_tensor(out=x1[:, tl, hf * 512:(hf + 1) * 512],
                                                               in0=mm[mi][:, :], in1=xt[xi][:, hf * 512:(hf + 1) * 512],
                                                               op=ALU.add),
                              [b_mm[mi], b_xt[xi]], [b_x1[tl]])
                prev = None
                for tl in range(4):
                    hi = norm_tile_A(x1[:, tl, :], b_x1[tl], 2)
                    if prev is not None:
                        norm_tile_B(prev[0], hn2T, b_hn2T, prev[1] * 128)
                    prev = (hi, tl)
                norm_tile_B(prev[0], hn2T, b_hn2T, prev[1] * 128)

            def ffn_phase(qd, mix_next=False):
                mix_hi = {}
                for jf in range(NJ):
                    if mix_next:
                        if jf % 5 == 1 and jf // 5 < 4:
                            mix_hi[jf // 5] = mix_A(qd + 1, jf // 5)
                        if jf % 5 == 4 and jf // 5 < 4:
                            norm_tile_B(mix_hi[jf // 5], mixT, b_mixT, (jf // 5) * 128)
                    wi = nxt("wgu", 2)
                    load_w_chunk(wg_sb[wi], b_wg[wi], w_gate_b, b_scr["w_gate_b"], jf * 128, 128)
                    load_w_chunk(wu_sb[wi], b_wu[wi], w_up_b, b_scr["w_up_b"], jf * 128, 128)
                    mg = nxt("mm", NMM)
                    cx.op("pe", [(lambda kc=kc: pe.matmul(mm[mg][:, :], lhsT=wg_sb[wi][:, kc, :], rhs=hn2T[:, kc, :],
                                                          start=(kc == 0), stop=(kc == 7))) for kc in range(8)],
                          [b_wg[wi], b_hn2T], [b_mm[mg]])
                    mu = nxt("mm", NMM)
                    cx.op("pe", [(lambda kc=kc: pe.matmul(mm[mu][:, :], lhsT=wu_sb[wi][:, kc, :], rhs=hn2T[:, kc, :],
                                                          start=(kc == 0), stop=(kc == 7))) for kc in range(8)],
                          [b_wu[wi], b_hn2T], [b_mm[mu]])
                    si = nxt("sg", 2)
                    cx.op("act", lambda: act.activation(out=sg[si][:], in_=mm[mg][:, :], func=AF.Silu),
                          [b_mm[mg]], [b_sg[si]])
                    cx.op("dve", lambda: dve.tensor_tensor(out=hT[:, jf, :], in0=mm[mu][:, :], in1=sg[si][:], op=ALU.mult),
                          [b_mm[mu], b_sg[si]], [b_hT[jf]])

            def down_phase(qd):
                def final_norm(yi, j):
                    st0 = stat_slot()
                    sd = rms_sd(yt[yi][:], b_yt[yi], D, st0)
                    rms_apply(yt[yi][:], b_yt[yi], yt[yi][:], b_yt[yi], sd, st0, gb[:, 3, :])
                    r0 = row0 + j * 128
                    cx.dma("pool", yout[r0:r0 + 128, :], yt[yi][:], [b_yt[yi]], [], b_yt[yi])

                prev = None
                for tl in range(4):
                    j = qd * 4 + tl
                    yi = nxt("yt", 2)
                    for hf in range(2):
                        mi = nxt("mm", NMM)
                        cx.op("pe", [(lambda jf=jf: pe.matmul(mm[mi][:, :], lhsT=hT[:, jf, tl * 128:(tl + 1) * 128],
                                                              rhs=wd_sb[:, jf, hf * 512:(hf + 1) * 512],
                                                              start=(jf == 0), stop=(jf == NJ - 1))) for jf in range(NJ)],
                              b_hT + [b_wd], [b_mm[mi]])
                        cx.op("dve", lambda: dve.tensor_tensor(out=yt[yi][:, hf * 512:(hf + 1) * 512], in0=mm[mi][:, :],
                                                               in1=x1[:, tl, hf * 512:(hf + 1) * 512], op=ALU.add),
                              [b_mm[mi], b_x1[tl]], [b_yt[yi]])
                    if prev is not None:
                        final_norm(*prev)
                    prev = (yi, j)
                final_norm(*prev)

            mix_phase(0)
            for qd in range(4):
                outproj_phase(qd)
                ffn_phase(qd, mix_next=(qd + 1 < 4))
                down_phase(qd)
        cx.barrier()
    return nc


_DIL_TABLE = None


def make_in_maps(x_prompt, x_sample, w_in, rpb, g_attn, g_na, g_dil, w_out, g_ffn, w_gate, w_up, w_down, g_final,
                 n_cores=8, n_sample=4, do_prompt=True):
    global _DIL_TABLE
    if _DIL_TABLE is None:
        _DIL_TABLE = host_dil_table()
    f = lambda a: np.ascontiguousarray(np.asarray(a, dtype=np.float32))
    gvec = np.stack([f(g_attn)[0], np.concatenate([f(g_na)[0], f(g_dil)[0]]), f(g_ffn)[0], f(g_final)], axis=0)
    rpb0 = f(rpb)[0]
    bna_s = host_na_table(rpb0, "s", 0)
    bna_p = [host_na_table(rpb0, "p", 0), host_na_table(rpb0, "p", 1)]
    xs = f(x_sample)
    xp = f(x_prompt)
    maps = []
    for c in range(n_cores):
        rows = [xs[n_sample * c + i] for i in range(n_sample)]
        half = c % 2
        if do_prompt:
            s = c // 2
            rows.append(xp[s, half * UT:(half + 1) * UT])
            rows.append(xp[s, UT:UT + HALO] if half == 0 else xp[s, HALO:UT])
        g = np.zeros((128, 2), np.float32)
        g[:, half] = 1.0
        maps.append({
            "xin": np.ascontiguousarray(np.concatenate(rows, axis=0)),
            "w_in": f(w_in)[0], "w_out": f(w_out)[0], "w_gate": f(w_gate)[0], "w_up": f(w_up)[0],
            "w_down": f(w_down)[0], "gv": gvec, "bdil": _DIL_TABLE, "bna_s": bna_s, "bna_p": bna_p[half],
            "gates": g,
        })
    return maps


_NC_CACHE = {}


def kernel(x_prompt, x_sample, w_in, rpb, g_attn, g_na, g_dil, w_out, g_ffn, w_gate, w_up, w_down, g_final):
    if "nc" not in _NC_CACHE:
        _NC_CACHE["nc"] = build_program(4, True)
    nc = _NC_CACHE["nc"]
    maps = make_in_maps(x_prompt, x_sample, w_in, rpb, g_attn, g_na, g_dil, w_out, g_ffn, w_gate, w_up, w_down, g_final)
    res = run_bass_kernel_spmd(nc, maps, core_ids=list(range(8)))
    y_sample = np.empty((32, 2048, D), np.float32)
    y_prompt = np.empty((4, 4096, D), np.float32)
    for c in range(8):
        y = np.asarray(res.results[c]["yout"], dtype=np.float32)
        for i in range(4):
            y_sample[4 * c + i] = y[i * UT:(i + 1) * UT]
        y_prompt[c // 2, (c % 2) * UT:(c % 2 + 1) * UT] = y[4 * UT:5 * UT]
    return (y_prompt, y_sample)
```

```python
import contextlib
import numpy as np
import concourse.bass as bass
import concourse.mybir as mybir
from concourse.bass_utils import run_bass_kernel_spmd

F32 = mybir.dt.float32
BF16 = mybir.dt.bfloat16
AF = mybir.ActivationFunctionType
ALU = mybir.AluOpType

D = 1024
DFF = 2816
NJ = DFF // 128
UT = 2048
HALO = 1024
NEGB = -30000.0
EPS = 1e-6
N_DIL_T = 17
NPW = 7
NPE = 4
NHB = 3
NXT = 3
NA_INT = 5


def na_special_pairs(kind):
    if kind == "s":
        return [(0, k) for k in range(4)] + [(1, k) for k in range(4)] + \
               [(14, k) for k in range(12, 16)] + [(15, k) for k in range(12, 16)]
    return [(0, k) for k in (0, 1, 2, 3, 22, 23)] + [(1, k) for k in (0, 1, 2, 3, 23)] + \
           [(14, k) for k in (12, 13, 14, 15, 16)] + [(15, k) for k in (12, 13, 14, 15, 16, 17)]


def na_ntypes(kind):
    return NA_INT + len(na_special_pairs(kind))


def na_steps(kind):
    sp = na_special_pairs(kind)
    sp_idx = {p: NA_INT + i for i, p in enumerate(sp)}
    nslots = 16 if kind == "s" else 24
    chunks = []
    for j0 in range(0, 16, 4):
        steps = []
        js = list(range(j0, j0 + 4))
        for j in js:
            if j in (0, 1, 14, 15):
                for (jj, k) in sp:
                    if jj == j:
                        steps.append((k, j, j + 1, sp_idx[(j, k)], None))
        interior = [j for j in js if j not in (0, 1, 14, 15)]
        if interior:
            for k in range(nslots):
                v = [j for j in interior if abs(k - j) <= 2 and k < 16]
                if v:
                    ja, jb = v[0], v[-1] + 1
                    steps.append((k, ja, jb, 2 - (k - ja), None))
        chunks.append((j0, steps))
    return chunks


def dil_steps(kind):
    chunks = []
    for j0 in range(0, 16, 4):
        steps = []
        js = list(range(j0, j0 + 4))
        for k in range(16):
            v = [j for j in js if abs(k - j) <= 8]
            if v:
                ja, jb = v[0], v[-1] + 1
                steps.append((k, ja, jb, 8 - (k - ja), None))
        if kind == "p":
            for s in range(16, 24):
                v = [j for j in js if j >= 8 and s <= j + 8]
                if v:
                    ja, jb = v[0], v[-1] + 1
                    steps.append((s, ja, jb, 8 - s + ja, 0))
                v = [j for j in js if j <= 7 and s >= j + 16]
                if v:
                    ja, jb = v[0], v[-1] + 1
                    steps.append((s, ja, jb, 32 - s + ja, 1))
        chunks.append((j0, steps))
    return chunks


def host_dil_table():
    slopes = np.array([2.0 ** (-(i + 1)) for i in range(8)], dtype=np.float64)
    k = np.arange(128)[:, None, None]
    i = np.arange(N_DIL_T)[None, :, None]
    q = np.arange(128)[None, None, :]
    delta = 128 * (8 - i) + k - q
    ad = np.abs(delta)
    mult = (ad <= 64).astype(np.float64) + ((delta % 4 == 0) & (ad <= 256)) + ((delta % 16 == 0) & (ad <= 1024))
    out = np.empty((8, 128, N_DIL_T, 128), dtype=np.float32)
    for h in range(8):
        out[h] = (mult * np.exp(-slopes[h] * ad)).astype(np.float32)
    return out.reshape(8, 128, N_DIL_T * 128)


def _na_tile(rpb, R, qrows, krows):
    out = np.full((8, 128, 128), NEGB, dtype=np.float32)
    kc = np.arange(64)[:, None]
    qc = np.arange(64)[None, :]
    ws = np.clip(qc - 8, 0, 48)
    ok = (kc >= ws) & (kc < ws + 16)
    rel = np.clip(kc - qc + 15, 0, 30)
    for ki, kr in enumerate(krows):
        for qi, qr in enumerate(qrows):
            if kr is None or kr < 0 or kr >= R:
                continue
            rs = int(np.clip(qr - 4, 0, R - 8))
            if not (rs <= kr < rs + 8):
                continue
            roff = kr - qr + 7
            g = rpb[:, roff][:, rel]
            out[:, ki * 64:(ki + 1) * 64, qi * 64:(qi + 1) * 64] = np.where(ok[None], g, np.float32(NEGB))
    return out


def host_na_table(rpb, kind, half):
    nt = na_ntypes(kind)
    out = np.empty((8, 128, nt, 128), dtype=np.float32)
    for i in range(NA_INT):
        dlt = 2 - i
        out[:, :, i, :] = _na_tile(rpb, 64, (16, 17), (16 + 2 * dlt, 17 + 2 * dlt))
    for n, (j, slot) in enumerate(na_special_pairs(kind)):
        if kind == "s":
            R = 32
            qrows = (2 * j, 2 * j + 1)
            krows = (2 * slot, 2 * slot + 1)
        else:
            R = 64
            if half == 0:
                qrows = (2 * j, 2 * j + 1)
                krows = (2 * slot, 2 * slot + 1)
            else:
                qrows = (32 + 2 * j, 33 + 2 * j)
                if slot < 16:
                    krows = (32 + 2 * slot, 33 + 2 * slot)
                else:
                    g = slot - 8
                    krows = (2 * g, 2 * g + 1)
        out[:, :, NA_INT + n, :] = _na_tile(rpb, R, qrows, krows)
    return out.reshape(8, 128, nt * 128)


class Buf:
    __slots__ = ("name", "w", "r", "dsem", "dcnt", "dma")

    def __init__(self, name):
        self.name = name
        self.w = None
        self.r = {}
        self.dsem = {}
        self.dcnt = {}
        self.dma = False


class Ctx:
    def __init__(self, nc, es):
        self.nc = nc
        self.es = es
        self.eng = {"pe": nc.tensor, "act": nc.scalar, "dve": nc.vector, "pool": nc.gpsimd, "sp": nc.sync}
        self.sem = {e: es.enter_context(nc.semaphore("s_" + e)) for e in self.eng}
        self.cnt = {e: 0 for e in self.eng}
        self.seen = {e: {} for e in self.eng}
        self.semname = {}
        self.dma_toks = []
        self.nsem = 0
        self.defer = None
        self.snap = {}

    def buf(self, name, dma=False):
        b = Buf(name)
        b.dma = dma
        return b

    def _wait(self, e, tok, raw):
        if tok is None:
            return
        te, sem, val = tok
        if te == e:
            if e == "pe" or e == "sp" or (not raw and e != "pool"):
                return
        key = id(sem)
        if self.seen[e].get(key, 0) >= val:
            return
        self.seen[e][key] = val
        snap = self.snap.get((key, val))
        if snap is not None:
            se = self.seen[e]
            for k_, v_ in snap.items():
                if se.get(k_, 0) < v_:
                    se[k_] = v_
        if self.defer is not None:
            self.defer.append((sem, val))
        else:
            self.eng[e].wait_ge(sem, val)

    def _sync(self, e, reads, writes):
        for b in reads:
            self._wait(e, b.w, True)
        for b in writes:
            self._wait(e, b.w, False)
            for t in b.r.values():
                self._wait(e, t, False)

    def _mark(self, tok, reads, writes):
        for b in reads:
            b.r[id(tok[1])] = tok
        for b in writes:
            b.w = tok
            b.r = {}

    def op(self, e, fns, reads=(), writes=()):
        self.defer = []
        self._sync(e, reads, writes)
        pend, self.defer = self.defer, None
        for (sem_, val_) in pend[:-1]:
            self.eng[e].wait_ge(sem_, val_)
        if not isinstance(fns, (list, tuple)):
            fns = [fns]
        ins = None
        for i_, f in enumerate(fns):
            ins = f()
            if i_ == 0 and pend:
                ins._wait_ge(pend[-1][0], pend[-1][1])
        self.cnt[e] += 1
        ins.then_inc(self.sem[e], 1)
        tok = (e, self.sem[e], self.cnt[e])
        if e != "pool":
            self.snap[(id(self.sem[e]), self.cnt[e])] = dict(self.seen[e])
        self._mark(tok, reads, writes)
        return tok

    def dma(self, q, out_ap, in_ap, reads, writes, sembuf, waw=True, track=True):
        if waw:
            self._sync(q, reads, writes)
        else:
            self._sync(q, reads, [])
        cls = "sw" if q == "pool" else "hw"
        if cls not in sembuf.dsem:
            sembuf.dsem[cls] = self.es.enter_context(self.nc.semaphore("d%s_%s_%d" % (cls, sembuf.name, self.nsem)))
            sembuf.dcnt[cls] = 0
            self.nsem += 1
        sem = sembuf.dsem[cls]
        self.eng[q].dma_start(out=out_ap, in_=in_ap).then_inc(sem, 16)
        sembuf.dcnt[cls] += 16
        tok = ("dma", sem, sembuf.dcnt[cls])
        self._mark(tok, reads, writes)
        if track:
            self.dma_toks.append(tok)
        return tok

    def barrier(self):
        toks = [(e, self.sem[e], self.cnt[e]) for e in self.eng if self.cnt[e] > 0]
        last = {}
        for t in self.dma_toks:
            k = id(t[1])
            if k not in last or last[k][2] < t[2]:
                last[k] = t
        toks += list(last.values())
        self.dma_toks = list(last.values())
        for e in self.eng:
            for t in toks:
                if t[0] == e:
                    continue
                key = id(t[1])
                if self.seen[e].get(key, 0) >= t[2]:
                    continue
                self.seen[e][key] = t[2]
                self.eng[e].wait_ge(t[1], t[2])


def build_program(n_sample=4, do_prompt=True, debug=False):
    nc = bass.Bass("TRN2", target_bir_lowering=False)
    units = ["s"] * n_sample + (["p"] if do_prompt else [])
    nunits = len(units)
    n_in_rows = nunits * UT + (HALO if do_prompt else 0)

    def din(name, shape, dt=F32):
        return nc.dram_tensor(name, list(shape), dt, kind="ExternalInput").ap()

    def dscr(name, shape, dt=BF16):
        return nc.dram_tensor(name, list(shape), dt, kind="Internal").ap()

    xin = din("xin", [n_in_rows, D])
    w_in = din("w_in", [D, 3 * D])
    w_out = din("w_out", [D, D])
    w_gate = din("w_gate", [D, DFF])
    w_up = din("w_up", [D, DFF])
    w_down = din("w_down", [DFF, D])
    gv = din("gv", [4, D])
    bdil = din("bdil", [8, 128, N_DIL_T * 128])
    nts, ntp = na_ntypes("s"), na_ntypes("p")
    bna_s = din("bna_s", [8, 128, nts * 128])
    bna_p = din("bna_p", [8, 128, ntp * 128])
    gates_in = din("gates", [128, 2])
    yout = nc.dram_tensor("yout", [nunits * UT, D], F32, kind="ExternalOutput").ap()

    w_in_b = dscr("w_in_b", [D, 3 * D])
    w_out_b = dscr("w_out_b", [D, D])
    w_gate_b = dscr("w_gate_b", [D, DFF])
    w_up_b = dscr("w_up_b", [D, DFF])
    w_down_b = dscr("w_down_b", [DFF, D])
    wdil = dscr("wdil", [8, 128, N_DIL_T * 128])
    wna_s = dscr("wna_s", [8, 128, nts * 128])
    wna_p = dscr("wna_p", [8, 128, ntp * 128])
    NTMAX = max(nts, ntp)

    es = contextlib.ExitStack()
    with es:
        cx = Ctx(nc, es)
        pe, act, dve, pool, sp = nc.tensor, nc.scalar, nc.vector, nc.gpsimd, nc.sync

        def sb(name, shape, dt):
            return es.enter_context(nc.sbuf_tensor(name, list(shape), dt))

        def ps(name, shape, dt):
            return es.enter_context(nc.psum_tensor(name, list(shape), dt))

        gb = sb("gb", [128, 4, D], F32)
        ident = sb("ident", [128, 128], BF16)
        identf = sb("identf", [128, 128], F32)
        gates = sb("gates_sb", [128, 2], F32)
        xt = [sb("xt%d" % i, [128, D], F32) for i in range(NXT)]
        hb = [sb("hb%d" % i, [128, D], BF16) for i in range(NHB)]
        o_sb = sb("o_sb", [128, 16, D], BF16)
        junk = sb("junk", [128, D], BF16)
        yt = [sb("yt%d" % i, [128, D], F32) for i in range(2)]
        stat = sb("stat", [128, 64], F32)
        eps_t = sb("eps_t", [128, 1], F32)
        rden = [sb("rden%d" % i, [128, 4], F32) for i in range(2)]
        REG_ELEMS = 65536
        reg = sb("region", [128, REG_ELEMS], BF16)
        regpos = [0]

        def carve(shape_free, dt):
            n = int(np.prod(shape_free))
            ne = n * (2 if dt == F32 else 1)
            off = regpos[0]
            off = (off + 1) // 2 * 2
            assert off + ne <= REG_ELEMS, (off, ne)
            regpos[0] = off + ne
            ap = reg[:, off:off + ne]
            if dt == F32:
                ap = ap.bitcast(F32)
            if len(shape_free) == 2:
                ap = ap.rearrange("p (a b) -> p a b", a=shape_free[0])
            elif len(shape_free) == 3:
                ap = ap.rearrange("p (a b c) -> p a b c", a=shape_free[0], b=shape_free[1])
            return ap

        regpos[0] = 0
        hnT = carve([8, UT + HALO], BF16)
        Vflat = carve([24 * 8, 65], BF16)
        Vsb = Vflat.rearrange("p (a b) c -> p a b c", a=24)
        qtm_off = (regpos[0] + 1) // 2 * 2
        QTm = [carve([UT], BF16) for _ in range(2)]
        KT1 = carve([UT + HALO], BF16)
        kt_end = regpos[0]
        wv_sb = carve([8, 512], BF16)
        wq_sb = [carve([8, 128], BF16) for _ in range(2)]
        wk_sb = [carve([8, 128], BF16) for _ in range(2)]
        wtab = [carve([NTMAX * 128], BF16) for _ in range(2)]
        pexp = [carve([512], BF16) for _ in range(NPE)]
        pw = [carve([512], BF16) for _ in range(NPW)]
        endA = regpos[0]
        regpos[0] = 0
        hnT_f32 = carve([8, (UT + HALO) // 2], F32)
        regpos[0] = qtm_off
        btmp = carve([NTMAX * 128], F32)
        assert regpos[0] <= kt_end
        regpos[0] = 0
        wout_sb = carve([8, D], BF16)
        wd_sb = carve([NJ, D], BF16)
        x1 = carve([4, D], F32)
        mixT = carve([8, 512], BF16)
        hn2T = carve([8, 512], BF16)
        hT = carve([NJ, 512], BF16)
        wg_sb = [carve([8, 128], BF16) for _ in range(2)]
        wu_sb = [carve([8, 128], BF16) for _ in range(2)]
        sg = [carve([512], F32) for _ in range(2)]
        endB = regpos[0]

        NMM = 6
        mm = [ps("mm%d" % i, [128, 512], F32) for i in range(NMM)]
        tp = [m[:, :].bitcast(BF16).rearrange("p (c n) -> p c n", c=8) for m in mm]
        pv = [ps("pv%d" % i, [128, 4, 128], F32) for i in range(2)]

        B = cx.buf
        b_gb = B("gb", dma=True)
        b_ident = B("ident")
        b_gates = B("gates", dma=True)
        b_xt = [B("xt%d" % i, dma=True) for i in range(NXT)]
        b_hb = [B("hb%d" % i) for i in range(NHB)]
        b_o = [B("o%d" % i) for i in range(16)]
        b_junk = B("junk")
        b_yt = [B("yt%d" % i, dma=True) for i in range(2)]
        b_stats = [B("stat%d" % i) for i in range(8)]
        b_rden = [B("rden%d" % i) for i in range(2)]
        b_hnT = [B("hnT%d" % i) for i in range(6)]
        b_V = [B("V%d" % i) for i in range(24)]
        b_QTm = [B("QTm%d" % i) for i in range(2)]
        b_KT1 = B("KT1")
        b_wv = B("wv", dma=True)
        b_wq = [B("wq%d" % i, dma=True) for i in range(2)]
        b_wk = [B("wk%d" % i, dma=True) for i in range(2)]
        b_wtab = [B("wtab%d" % i, dma=True) for i in range(2)]
        b_pexp = [B("pexp%d" % i) for i in range(NPE)]
        b_pw = [B("pw%d" % i) for i in range(NPW)]
        b_tabld = B("tabld", dma=True)
        b_eps = B("eps")
        b_wout = B("wout", dma=True)
        b_wd = B("wd", dma=True)
        b_x1 = [B("x1_%d" % i) for i in range(4)]
        b_mixT = B("mixT")
        b_hn2T = B("hn2T")
        b_hT = [B("hT%d" % i) for i in range(NJ)]
        b_wg = [B("wg%d" % i, dma=True) for i in range(2)]
        b_wu = [B("wu%d" % i, dma=True) for i in range(2)]
        b_sg = [B("sg%d" % i) for i in range(2)]
        b_mm = [B("mm%d" % i) for i in range(NMM)]
        b_tp = b_mm
        b_pv = [B("pv%d" % i) for i in range(2)]
        b_scr = {n: B(n, dma=True) for n in ("w_out_b", "w_gate_b", "w_up_b", "w_down_b")}
        b_win = [B("w_in_b%d" % i, dma=True) for i in range(6)]
        gate_t = sb("gate_t", [128, 2], F32)
        b_gate_t = B("gate_t")
        b_wdil = [B("wdil%d" % h, dma=True) for h in range(8)]
        b_wnas = [B("wnas%d" % h, dma=True) for h in range(8)]
        b_wnap = [B("wnap%d" % h, dma=True) for h in range(8)]

        rr = {"mm": 0, "tp": 0, "xt": 0, "hb": 0, "yt": 0, "pe": 0, "pw": 0, "pv": 0, "rden": 0, "wt": 0,
              "qk": 0, "wgu": 0, "sg": 0, "st": 0}

        def nxt(key, n):
            i = rr[key]
            rr[key] = (i + 1) % n
            return i

        def stat_slot():
            i = nxt("st", 8)
            return i * 8

        def cast_weight(name, src, dst):
            nr = src.shape[0]
            for r0 in range(0, nr, 256):
                r1 = min(nr, r0 + 256)
                cx.dma("pool", dst[r0:r1, :], src[r0:r1, :], [], [b_scr[name]], b_scr[name], waw=False, track=False)

        def cast_win_block(i):
            for r0 in range(0, D, 256):
                cx.dma("pool", w_in_b[r0:r0 + 256, i * 512:(i + 1) * 512], w_in[r0:r0 + 256, i * 512:(i + 1) * 512],
                       [], [b_win[i]], b_win[i], waw=False, track=False)

        def pool_gate(bufs):
            cx.op("pool", lambda: pool.memset(gate_t[:], 0.0), bufs, [b_gate_t])

        for i in (2, 0, 1):
            cast_win_block(i)
        for i in range(4):
            cx.dma("sp", gb[:, i, :], gv[i:i + 1, :].partition_broadcast(128), [], [b_gb], b_gb, waw=False)
        cx.dma("sp", gates[:], gates_in[:, :], [], [b_gates], b_gates)
        cx.op("pool", lambda: pool.memset(identf[:], 1.0), [], [b_ident])
        cx.op("pool", lambda: pool.affine_select(out=identf[:], in_=identf[:], pattern=[[-1, 128]],
                                                compare_op=ALU.is_equal, fill=0.0, base=0, channel_multiplier=1),
              [b_ident], [b_ident])
        cx.op("dve", lambda: dve.tensor_copy(ident[:], identf[:]), [b_ident], [b_ident])
        cx.op("dve", lambda: dve.memset(eps_t[:], EPS), [], [b_eps])

        b_alias = [b_QTm[0], b_QTm[1], b_KT1]

        def build_table_head(src, dst, bl, ncols, h):
            cx.dma("sp", btmp[:, 0:ncols], src[h], [], b_alias, b_tabld)
            wb = wtab[h % 2]
            cx.op("act", (lambda wb=wb: act.activation(out=wb[:, 0:ncols], in_=btmp[:, 0:ncols], func=AF.Exp)),
                  b_alias, [b_wtab[h % 2]])
            cx.dma("sp", dst[h], wb[:, 0:ncols], [b_wtab[h % 2]], [bl[h]], bl[h])

        built = {"s": False, "p": False, "casts": False}
        cast_jobs = []
        cast_gate = [None]
        def rms_sd(src_ap, b_src, ncols, st0):
            b_stat = b_stats[st0 // 8]
            cx.op("act", lambda: act.activation(out=junk[:, 0:ncols], in_=src_ap, func=AF.Square,
                                                accum_out=stat[:, st0:st0 + 1]),
                  [b_src], [b_junk, b_stat])
            cx.op("act", lambda: act.activation(out=stat[:, st0 + 1:st0 + 2], in_=stat[:, st0:st0 + 1],
                                                func=AF.Sqrt, scale=1.0 / ncols, bias=eps_t[:, 0:1]),
                  [b_stat, b_eps], [b_stat])
            return stat[:, st0 + 1:st0 + 2]

        def rms_apply(out_ap, b_out, src_ap, b_src, sd, st0, g_ap):
            b_stat = b_stats[st0 // 8]
            cx.op("dve", lambda: dve.reciprocal(out=stat[:, st0 + 2:st0 + 3], in_=sd), [b_stat], [b_stat])
            cx.op("dve", lambda: dve.scalar_tensor_tensor(out=out_ap, in0=src_ap, scalar=stat[:, st0 + 2:st0 + 3], in1=g_ap,
                                                          op0=ALU.mult, op1=ALU.mult),
                  [b_src, b_stat, b_gb], [b_out])

        def transpose_to(src_bf, b_src, dstT, b_dst, col0):
            ti = nxt("mm", NMM)
            cx.op("pe", [(lambda c=c: pe.transpose(tp[ti][:, c, :], src_bf[:, c * 128:(c + 1) * 128], ident[:]))
                         for c in range(8)],
                  [b_src, b_ident], [b_tp[ti]])
            cx.op("act", lambda: act.copy(out=dstT[:, :, col0:col0 + 128], in_=tp[ti]),
                  [b_tp[ti]], [b_dst])

        def norm_tile_A(src_ap, b_src, gidx):
            st0 = stat_slot()
            sd = rms_sd(src_ap, b_src, D, st0)
            hi = nxt("hb", NHB)
            rms_apply(hb[hi][:], b_hb[hi], src_ap, b_src, sd, st0, gb[:, gidx, :])
            return hi

        def norm_tile_B(hi, dstT, b_dst, col0):
            transpose_to(hb[hi], b_hb[hi], dstT, b_dst, col0)

        def load_w_chunk(dst, b_dst, src_scr, b_src, col0, ncol, q="sp"):
            if src_scr is w_in_b:
                b_src = b_win[col0 // 512]
            cx.dma(q, dst[:, :, 0:ncol], src_scr[:, col0:col0 + ncol].rearrange("(c p) n -> p c n", p=128),
                   [b_src], [b_dst], b_dst)

        def proj_T(w_sb, b_w, srcT, b_srcs, col0, n, dst_ap, b_dst):
            mi = nxt("mm", NMM)
            cx.op("pe", [(lambda kc=kc: pe.matmul(mm[mi][:, 0:n], lhsT=w_sb[:, kc, :], rhs=srcT[:, kc, col0:col0 + n],
                                                  start=(kc == 0), stop=(kc == 7))) for kc in range(8)],
                  [b_w] + b_srcs, [b_mm[mi]])
            cx.op("dve", lambda: dve.tensor_copy(dst_ap, mm[mi][:, 0:n]), [b_mm[mi]], [b_dst])

        PV_DEPTH = 4
        pvq = []

        def pv_emit(ent):
            (k, ja, jb, pwi, pvi, j0, hv, hcol, first, last) = ent
            fns = []
            for n_, j in enumerate(range(ja, jb)):
                st_flag = first and n_ == 0
                fns.append(lambda n_=n_, j=j, st_flag=st_flag: pe.matmul(
                    pv[pvi][:, j - j0, 0:65], lhsT=pw[pwi][:, n_ * 128:(n_ + 1) * 128],
                    rhs=Vsb[:, k, hv, :], start=st_flag, stop=False, skip_group_check=True))
            cx.op("pe", fns, [b_pw[pwi], b_V[k]], [b_pv[pvi]])
            if last:
                ri = nxt("rden", 2)
                cx.op("dve", lambda: dve.reciprocal(out=rden[ri][:], in_=pv[pvi][:, :, 64]),
                      [b_pv[pvi]], [b_rden[ri]])
                cx.op("dve", lambda: dve.tensor_tensor(out=o_sb[:, j0:j0 + 4, hcol:hcol + 64], in0=pv[pvi][:, :, 0:64],
                                                       in1=rden[ri][:].unsqueeze(2).to_broadcast([128, 4, 64]),
                                                       op=ALU.mult),
                      [b_pv[pvi], b_rden[ri]], [b_o[j] for j in range(j0, j0 + 4)])

        def pv_flush(keep=0):
            while len(pvq) > keep:
                pv_emit(pvq.pop(0))

        def attention_head(kind, chunks, qt_ap, kt_ap, b_qt, b_kt, hcol, hv, wt_ap, b_wt):
            for (j0, steps) in chunks:
                pvi = nxt("pv", 2)
                ns = len(steps)
                for si_, (k, ja, jb, ti, gate) in enumerate(steps):
                    n = (jb - ja) * 128
                    mi = nxt("mm", NMM)
                    cx.op("pe", lambda: pe.matmul(mm[mi][:, 0:n], lhsT=kt_ap[:, k * 128:(k + 1) * 128],
                                                  rhs=qt_ap[:, ja * 128:jb * 128], start=True, stop=True),
                          [b_kt, b_qt], [b_mm[mi]])
                    pv_flush(PV_DEPTH)
                    pei = nxt("pe", NPE)
                    cx.op("act", lambda: act.activation(out=pexp[pei][:, 0:n], in_=mm[mi][:, 0:n], func=AF.Exp,
                                                        scale=0.125),
                          [b_mm[mi]], [b_pexp[pei]])
                    pwi = nxt("pw", NPW)
                    if gate is None:
                        cx.op("dve", lambda: dve.tensor_tensor(out=pw[pwi][:, 0:n], in0=pexp[pei][:, 0:n],
                                                               in1=wt_ap[:, ti * 128:ti * 128 + n], op=ALU.mult),
                              [b_pexp[pei], b_wt], [b_pw[pwi]])
                    else:
                        cx.op("dve", lambda: dve.scalar_tensor_tensor(out=pw[pwi][:, 0:n], in0=pexp[pei][:, 0:n],
                                                                      scalar=gates[:, gate:gate + 1],
                                                                      in1=wt_ap[:, ti * 128:ti * 128 + n],
                                                                      op0=ALU.mult, op1=ALU.mult),
                              [b_pexp[pei], b_wt, b_gates], [b_pw[pwi]])
                    pvq.append((k, ja, jb, pwi, pvi, j0, hv, hcol, si_ == 0, si_ == ns - 1))

        steps_cache = {("na", "s"): na_steps("s"), ("na", "p"): na_steps("p"),
                       ("dil", "s"): dil_steps("s"), ("dil", "p"): dil_steps("p")}
        for u, kind in enumerate(units):
            row0 = u * UT
            nkt = 24 if kind == "p" else 16
            ntok = nkt * 128

            def xrows(t):
                if t < 16:
                    return row0 + t * 128
                return nunits * UT + (t - 16) * 128

            def vproj(t):
                mi = nxt("mm", NMM)
                cx.op("pe", [(lambda kc=kc: pe.matmul(mm[mi][:, :], lhsT=hnT[:, kc, t * 128:(t + 1) * 128],
                                                      rhs=wv_sb[:, kc, :], start=(kc == 0), stop=(kc == 7)))
                             for kc in range(8)],
                      [b_wv, b_hnT[t // 4]], [b_mm[mi]])
                cx.op("dve", lambda: dve.tensor_copy(Vsb[:, t, :, 0:64],
                                                     mm[mi][:, :].rearrange("p (h d) -> p h d", h=8)),
                      [b_mm[mi]], [b_V[t]])

            if u > 0:
                cx.barrier()
            cx.op("dve", lambda: dve.memset(Vflat[:, :, 64:65], 1.0), [], b_V)
            wq_ = "pool" if u == 0 else "sp"
            load_w_chunk(wv_sb, b_wv, w_in_b, None, 1024, 512, q=wq_)
            qi0 = nxt("qk", 2)
            load_w_chunk(wq_sb[qi0], b_wq[qi0], w_in_b, None, 0, 128, q=wq_)
            load_w_chunk(wk_sb[qi0], b_wk[qi0], w_in_b, None, 512, 128, q=wq_)
            early_qk = built[kind]

            def zero_q_halves():
                cx.op("pool", lambda: pool.memset(QTm[0][64:128, :], 0.0), [], [b_QTm[0]])
                cx.op("pool", lambda: pool.memset(QTm[1][0:64, :], 0.0), [], [b_QTm[1]])

            def qproj(qi, tq):
                mi = nxt("mm", NMM)
                cx.op("pe", [(lambda kc=kc: pe.matmul(mm[mi][:, :], lhsT=wq_sb[qi][:, kc, :],
                                                      rhs=hnT[:, kc, tq * 512:(tq + 1) * 512],
                                                      start=(kc == 0), stop=(kc == 7))) for kc in range(8)],
                      [b_wq[qi], b_hnT[tq]], [b_mm[mi]])
                cx.op("dve", lambda: dve.tensor_copy(QTm[0][0:64, tq * 512:(tq + 1) * 512], mm[mi][0:64, :]),
                      [b_mm[mi]], [b_QTm[0]])
                cx.op("dve", lambda: dve.tensor_copy(QTm[1][64:128, tq * 512:(tq + 1) * 512], mm[mi][64:128, :]),
                      [b_mm[mi]], [b_QTm[1]])

            def kproj(qi, tk):
                proj_T(wk_sb[qi], b_wk[qi], hnT, [b_hnT[tk]], tk * 512, 512, KT1[:, tk * 512:(tk + 1) * 512], b_KT1)

            if early_qk:
                zero_q_halves()
            def after_tile(t):
                norm_tile_B(prev[0], hnT, b_hnT[t // 4], t * 128)
                vproj(t)
                if early_qk and t % 4 == 3:
                    if t // 4 < 4:
                        qproj(qi0, t // 4)
                    kproj(qi0, t // 4)
                if not built[kind] and t % 2 == 1 and t < 16:
                    if kind == "s":
                        build_table_head(bna_s, wna_s, b_wnas, nts * 128, t // 2)
                    else:
                        build_table_head(bna_p, wna_p, b_wnap, ntp * 128, t // 2)

            prev = None
            for t in range(nkt):
                xi = nxt("xt", NXT)
                r0 = xrows(t)
                cx.dma("sp", xt[xi][:], xin[r0:r0 + 128, :], [], [b_xt[xi]], b_xt[xi])
                hi = norm_tile_A(xt[xi][:], b_xt[xi], 0)
                if prev is not None:
                    after_tile(prev[1])
                prev = (hi, t)
            after_tile(prev[1])
            if not built[kind]:
                last_tab = b_wnas[7] if kind == "s" else b_wnap[7]
                built[kind] = True
                if not built["casts"]:
                    def _rows(name, src, dst, r0, r1):
                        def f():
                            for a in range(r0, r1, 256):
                                b_ = min(r1, a + 256)
                                cx.dma("pool", dst[a:b_, :], src[a:b_, :], [], [b_scr[name]], b_scr[name],
                                       waw=False, track=False)
                        return f

                    def _dil(h0):
                        def f():
                            for h in range(h0, h0 + 4):
                                cx.dma("pool", wdil[h], bdil[h], [], [b_wdil[h]], b_wdil[h], track=False)
                        return f

                    cast_jobs.extend([
                        (lambda: [cast_win_block(i) for i in (5, 3, 4)]),
                        _dil(0), _dil(4),
                        _rows("w_out_b", w_out, w_out_b, 0, D),
                        _rows("w_gate_b", w_gate, w_gate_b, 0, 512), _rows("w_gate_b", w_gate, w_gate_b, 512, D),
                        _rows("w_up_b", w_up, w_up_b, 0, 512), _rows("w_up_b", w_up, w_up_b, 512, D),
                        _rows("w_down_b", w_down, w_down_b, 0, 1408), _rows("w_down_b", w_down, w_down_b, 1408, DFF),
                    ])
                    cast_gate[0] = last_tab
                    built["casts"] = True
            if not early_qk:
                zero_q_halves()
            def release_casts(njobs, gate_bufs):
                if cast_jobs:
                    pool_gate(gate_bufs)
                    for _ in range(njobs):
                        if cast_jobs:
                            cast_jobs.pop(0)()

            if cast_jobs:
                release_casts(1, [cast_gate[0]])

            pairs = [(G, c) for G in range(2) for c in range(4)]
            pair_qi = {0: qi0}

            def load_pair(pi):
                G, c = pairs[pi]
                qi = nxt("qk", 2)
                load_w_chunk(wq_sb[qi], b_wq[qi], w_in_b, None, 1536 * G + c * 128, 128)
                load_w_chunk(wk_sb[qi], b_wk[qi], w_in_b, None, 1536 * G + 512 + c * 128, 128)
                pair_qi[pi] = qi

            head_wt = {}

            def load_wtab(hn):
                G, h = hn // 8, hn % 8
                wi = nxt("wt", 2)
                if G == 0:
                    src, bsrc, ncols = ((wna_s, b_wnas, nts * 128) if kind == "s" else (wna_p, b_wnap, ntp * 128))
                else:
                    src, bsrc, ncols = wdil, b_wdil, N_DIL_T * 128
                cx.dma("sp", wtab[wi][:, 0:ncols], src[h], [bsrc[h]], [b_wtab[wi]], b_wtab[wi])
                head_wt[hn] = wi

            bg_tab = (kind == "s" and not built["p"] and u + 1 < nunits and units[u + 1] == "p")
            PC = ntp * 128 // 16
            stage_f = hnT_f32[:, :, UT // 2:UT // 2 + PC]
            stage_b = hnT[:, :, UT + 2 * PC:UT + 3 * PC]
            b_stage = [b_hnT[4], b_hnT[5]]

            def bg_load(r):
                h, hf_ = r // 2, r % 2
                cx.dma("sp", stage_f, bna_p[h].rearrange("p (a b) -> p a b", a=16)[:, hf_ * 8:(hf_ + 1) * 8, :],
                       [], b_stage, b_tabld)

            def bg_exp_store(r):
                h, hf_ = r // 2, r % 2
                cx.op("act", lambda: act.activation(out=stage_b, in_=stage_f, func=AF.Exp), b_stage, b_stage)
                cx.dma("sp", wna_p[h].rearrange("p (a b) -> p a b", a=16)[:, hf_ * 8:(hf_ + 1) * 8, :], stage_b,
                       b_stage, [b_wnap[h]], b_wnap[h], waw=False)

            def bg_round(r):
                if bg_tab:
                    if r >= 1:
                        bg_exp_store(r - 1)
                    if r < 16:
                        bg_load(r)

            load_wtab(0)
            for pi, (G, c) in enumerate(pairs):
                bg_round(2 * pi)
                cb = 1536 * G
                gname = "na" if G == 0 else "dil"
                chunks = steps_cache[(gname, kind)]
                if G == 1 and c == 0:
                    pv_flush()
                    load_w_chunk(wv_sb, b_wv, w_in_b, None, cb + 1024, 512)
                    for t in range(nkt):
                        vproj(t)
                qi = pair_qi[pi]
                if not (pi == 0 and early_qk):
                    for tq in range(4):
                        qproj(qi, tq)
                    for tk in range(ntok // 512):
                        kproj(qi, tk)
                if pi >= 1:
                    release_casts(1 if pi < 6 else 2, [b_KT1])
                if pi + 1 < len(pairs):
                    load_pair(pi + 1)
                for hl in range(2):
                    hn = 2 * pi + hl
                    h = hn % 8
                    if hn + 1 < 16:
                        load_wtab(hn + 1)
                    wi = head_wt[hn]
                    if hl == 1:
                        bg_round(2 * pi + 1)
                    attention_head(kind, chunks, QTm[hl], KT1, b_QTm[hl], b_KT1,
                                   G * 512 + h * 64, h, wtab[wi], b_wtab[wi])
            if bg_tab:
                bg_round(16)
                built["p"] = True
            pv_flush()
            cx.barrier()
            cx.dma("sp", wout_sb[:], w_out_b.rearrange("(c p) n -> p c n", p=128), [b_scr["w_out_b"]], [b_wout], b_wout)
            cx.dma("pool", wd_sb[:], w_down_b.rearrange("(c p) n -> p c n", p=128), [b_scr["w_down_b"]], [b_wd], b_wd)

            def mix_A(qd, tl):
                j = qd * 4 + tl
                st0 = stat_slot()
                sa_ = rms_sd(o_sb[:, j, 0:512], b_o[j], 512, st0)
                st1 = stat_slot()
                sb_ = rms_sd(o_sb[:, j, 512:1024], b_o[j], 512, st1)
                hi = nxt("hb", NHB)
                rms_apply(hb[hi][:, 0:512], b_hb[hi], o_sb[:, j, 0:512], b_o[j], sa_, st0, gb[:, 1, 0:512])
                rms_apply(hb[hi][:, 512:1024], b_hb[hi], o_sb[:, j, 512:1024], b_o[j], sb_, st1, gb[:, 1, 512:1024])
                return hi

            def mix_phase(qd):
                prev = None
                for tl in range(4):
                    hi = mix_A(qd, tl)
                    if prev is not None:
                        norm_tile_B(prev[0], mixT, b_mixT, prev[1] * 128)
                    prev = (hi, tl)
                norm_tile_B(prev[0], mixT, b_mixT, prev[1] * 128)

            def outproj_phase(qd):
                for tl in range(4):
                    j = qd * 4 + tl
                    xi = nxt("xt", NXT)
                    r0 = row0 + j * 128
                    cx.dma("sp", xt[xi][:], xin[r0:r0 + 128, :], [], [b_xt[xi]], b_xt[xi])
                    for hf in range(2):
                        mi = nxt("mm", NMM)
                        cx.op("pe", [(lambda kc=kc: pe.matmul(mm[mi][:, :], lhsT=mixT[:, kc, tl * 128:(tl + 1) * 128],
                                                              rhs=wout_sb[:, kc, hf * 512:(hf + 1) * 512],
                                                              start=(kc == 0), stop=(kc == 7))) for kc in range(8)],
                              [b_mixT, b_wout], [b_mm[mi]])
                        cx.op("dve", lambda: dve.tensor_tensor(out=x1[:, tl, hf * 512:(hf + 1) * 512],
                                                               in0=mm[mi][:, :], in1=xt[xi][:, hf * 512:(hf + 1) * 512],
                                                               op=ALU.add),
                              [b_mm[mi], b_xt[xi]], [b_x1[tl]])
                prev = None
                for tl in range(4):
                    hi = norm_tile_A(x1[:, tl, :], b_x1[tl], 2)
                    if prev is not None:
                        norm_tile_B(prev[0], hn2T, b_hn2T, prev[1] * 128)
                    prev = (hi, tl)
                norm_tile_B(prev[0], hn2T, b_hn2T, prev[1] * 128)

            def ffn_phase(qd, mix_next=False):
                mix_hi = {}
                for jf in range(NJ):
                    if mix_next:
                        if jf % 5 == 1 and jf // 5 < 4:
                            mix_hi[jf // 5] = mix_A(qd + 1, jf // 5)
                        if jf % 5 == 4 and jf // 5 < 4:
                            norm_tile_B(mix_hi[jf // 5], mixT, b_mixT, (jf // 5) * 128)
                    wi = nxt("wgu", 2)
                    load_w_chunk(wg_sb[wi], b_wg[wi], w_gate_b, b_scr["w_gate_b"], jf * 128, 128)
                    load_w_chunk(wu_sb[wi], b_wu[wi], w_up_b, b_scr["w_up_b"], jf * 128, 128)
                    mg = nxt("mm", NMM)
                    cx.op("pe", [(lambda kc=kc: pe.matmul(mm[mg][:, :], lhsT=wg_sb[wi][:, kc, :], rhs=hn2T[:, kc, :],
                                                          start=(kc == 0), stop=(kc == 7))) for kc in range(8)],
                          [b_wg[wi], b_hn2T], [b_mm[mg]])
                    mu = nxt("mm", NMM)
                    cx.op("pe", [(lambda kc=kc: pe.matmul(mm[mu][:, :], lhsT=wu_sb[wi][:, kc, :], rhs=hn2T[:, kc, :],
                                                          start=(kc == 0), stop=(kc == 7))) for kc in range(8)],
                          [b_wu[wi], b_hn2T], [b_mm[mu]])
                    si = nxt("sg", 2)
                    cx.op("act", lambda: act.activation(out=sg[si][:], in_=mm[mg][:, :], func=AF.Silu),
                          [b_mm[mg]], [b_sg[si]])
                    cx.op("dve", lambda: dve.tensor_tensor(out=hT[:, jf, :], in0=mm[mu][:, :], in1=sg[si][:], op=ALU.mult),
                          [b_mm[mu], b_sg[si]], [b_hT[jf]])

            def down_phase(qd):
                def final_norm(yi, j):
                    st0 = stat_slot()
                    sd = rms_sd(yt[yi][:], b_yt[yi], D, st0)
                    rms_apply(yt[yi][:], b_yt[yi], yt[yi][:], b_yt[yi], sd, st0, gb[:, 3, :])
                    r0 = row0 + j * 128
                    cx.dma("pool", yout[r0:r0 + 128, :], yt[yi][:], [b_yt[yi]], [], b_yt[yi])

                prev = None
                for tl in range(4):
                    j = qd * 4 + tl
                    yi = nxt("yt", 2)
                    for hf in range(2):
                        mi = nxt("mm", NMM)
                        cx.op("pe", [(lambda jf=jf: pe.matmul(mm[mi][:, :], lhsT=hT[:, jf, tl * 128:(tl + 1) * 128],
                                                              rhs=wd_sb[:, jf, hf * 512:(hf + 1) * 512],
                                                              start=(jf == 0), stop=(jf == NJ - 1))) for jf in range(NJ)],
                              b_hT + [b_wd], [b_mm[mi]])
                        cx.op("dve", lambda: dve.tensor_tensor(out=yt[yi][:, hf * 512:(hf + 1) * 512], in0=mm[mi][:, :],
                                                               in1=x1[:, tl, hf * 512:(hf + 1) * 512], op=ALU.add),
                              [b_mm[mi], b_x1[tl]], [b_yt[yi]])
                    if prev is not None:
                        final_norm(*prev)
                    prev = (yi, j)
                final_norm(*prev)

            mix_phase(0)
            for qd in range(4):
                outproj_phase(qd)
                ffn_phase(qd, mix_next=(qd + 1 < 4))
                down_phase(qd)
        cx.barrier()
    return nc


_DIL_TABLE = None


def make_in_maps(x_prompt, x_sample, w_in, rpb, g_attn, g_na, g_dil, w_out, g_ffn, w_gate, w_up, w_down, g_final,
                 n_cores=8, n_sample=4, do_prompt=True):
    global _DIL_TABLE
    if _DIL_TABLE is None:
        _DIL_TABLE = host_dil_table()
    f = lambda a: np.ascontiguousarray(np.asarray(a, dtype=np.float32))
    gvec = np.stack([f(g_attn)[0], np.concatenate([f(g_na)[0], f(g_dil)[0]]), f(g_ffn)[0], f(g_final)], axis=0)
    rpb0 = f(rpb)[0]
    bna_s = host_na_table(rpb0, "s", 0)
    bna_p = [host_na_table(rpb0, "p", 0), host_na_table(rpb0, "p", 1)]
    xs = f(x_sample)
    xp = f(x_prompt)
    maps = []
    for c in range(n_cores):
        rows = [xs[n_sample * c + i] for i in range(n_sample)]
        half = c % 2
        if do_prompt:
            s = c // 2
            rows.append(xp[s, half * UT:(half + 1) * UT])
            rows.append(xp[s, UT:UT + HALO] if half == 0 else xp[s, HALO:UT])
        g = np.zeros((128, 2), np.float32)
        g[:, half] = 1.0
        maps.append({
            "xin": np.ascontiguousarray(np.concatenate(rows, axis=0)),
            "w_in": f(w_in)[0], "w_out": f(w_out)[0], "w_gate": f(w_gate)[0], "w_up": f(w_up)[0],
            "w_down": f(w_down)[0], "gv": gvec, "bdil": _DIL_TABLE, "bna_s": bna_s, "bna_p": bna_p[half],
            "gates": g,
        })
    return maps


_NC_CACHE = {}


def kernel(x_prompt, x_sample, w_in, rpb, g_attn, g_na, g_dil, w_out, g_ffn, w_gate, w_up, w_down, g_final):
    if "nc" not in _NC_CACHE:
        _NC_CACHE["nc"] = build_program(4, True)
    nc = _NC_CACHE["nc"]
    maps = make_in_maps(x_prompt, x_sample, w_in, rpb, g_attn, g_na, g_dil, w_out, g_ffn, w_gate, w_up, w_down, g_final)
    res = run_bass_kernel_spmd(nc, maps, core_ids=list(range(8)))
    y_sample = np.empty((32, 2048, D), np.float32)
    y_prompt = np.empty((4, 4096, D), np.float32)
    for c in range(8):
        y = np.asarray(res.results[c]["yout"], dtype=np.float32)
        for i in range(4):
            y_sample[4 * c + i] = y[i * UT:(i + 1) * UT]
        y_prompt[c // 2, (c % 2) * UT:(c % 2 + 1) * UT] = y[4 * UT:5 * UT]
    return (y_prompt, y_sample)
```

```python
import contextlib
import numpy as np
import concourse.bass as bass
import concourse.mybir as mybir
from concourse.bass_utils import run_bass_kernel_spmd

F32 = mybir.dt.float32
BF16 = mybir.dt.bfloat16
AF = mybir.ActivationFunctionType
ALU = mybir.AluOpType

D = 1024
DFF = 2816
NJ = DFF // 128
UT = 2048
HALO = 1024
NEGB = -30000.0
EPS = 1e-6
N_DIL_T = 17
NPW = 7
NPE = 4
NHB = 3
NXT = 3
NA_INT = 5


def na_special_pairs(kind):
    if kind == "s":
        return [(0, k) for k in range(4)] + [(1, k) for k in range(4)] + \
               [(14, k) for k in range(12, 16)] + [(15, k) for k in range(12, 16)]
    return [(0, k) for k in (0, 1, 2, 3, 22, 23)] + [(1, k) for k in (0, 1, 2, 3, 23)] + \
           [(14, k) for k in (12, 13, 14, 15, 16)] + [(15, k) for k in (12, 13, 14, 15, 16, 17)]


def na_ntypes(kind):
    return NA_INT + len(na_special_pairs(kind))


def na_steps(kind):
    sp = na_special_pairs(kind)
    sp_idx = {p: NA_INT + i for i, p in enumerate(sp)}
    nslots = 16 if kind == "s" else 24
    chunks = []
    for j0 in range(0, 16, 4):
        steps = []
        js = list(range(j0, j0 + 4))
        for j in js:
            if j in (0, 1, 14, 15):
                for (jj, k) in sp:
                    if jj == j:
                        steps.append((k, j, j + 1, sp_idx[(j, k)], None))
        interior = [j for j in js if j not in (0, 1, 14, 15)]
        if interior:
            for k in range(nslots):
                v = [j for j in interior if abs(k - j) <= 2 and k < 16]
                if v:
                    ja, jb = v[0], v[-1] + 1
                    steps.append((k, ja, jb, 2 - (k - ja), None))
        chunks.append((j0, steps))
    return chunks


def dil_steps(kind):
    chunks = []
    for j0 in range(0, 16, 4):
        steps = []
        js = list(range(j0, j0 + 4))
        for k in range(16):
            v = [j for j in js if abs(k - j) <= 8]
            if v:
                ja, jb = v[0], v[-1] + 1
                steps.append((k, ja, jb, 8 - (k - ja), None))
        if kind == "p":
            for s in range(16, 24):
                v = [j for j in js if j >= 8 and s <= j + 8]
                if v:
                    ja, jb = v[0], v[-1] + 1
                    steps.append((s, ja, jb, 8 - s + ja, 0))
                v = [j for j in js if j <= 7 and s >= j + 16]
                if v:
                    ja, jb = v[0], v[-1] + 1
                    steps.append((s, ja, jb, 32 - s + ja, 1))
        chunks.append((j0, steps))
    return chunks


def host_dil_table():
    slopes = np.array([2.0 ** (-(i + 1)) for i in range(8)], dtype=np.float64)
    k = np.arange(128)[:, None, None]
    i = np.arange(N_DIL_T)[None, :, None]
    q = np.arange(128)[None, None, :]
    delta = 128 * (8 - i) + k - q
    ad = np.abs(delta)
    mult = (ad <= 64).astype(np.float64) + ((delta % 4 == 0) & (ad <= 256)) + ((delta % 16 == 0) & (ad <= 1024))
    out = np.empty((8, 128, N_DIL_T, 128), dtype=np.float32)
    for h in range(8):
        out[h] = (mult * np.exp(-slopes[h] * ad)).astype(np.float32)
    return out.reshape(8, 128, N_DIL_T * 128)


def _na_tile(rpb, R, qrows, krows):
    out = np.full((8, 128, 128), NEGB, dtype=np.float32)
    kc = np.arange(64)[:, None]
    qc = np.arange(64)[None, :]
    ws = np.clip(qc - 8, 0, 48)
    ok = (kc >= ws) & (kc < ws + 16)
    rel = np.clip(kc - qc + 15, 0, 30)
    for ki, kr in enumerate(krows):
        for qi, qr in enumerate(qrows):
            if kr is None or kr < 0 or kr >= R:
                continue
            rs = int(np.clip(qr - 4, 0, R - 8))
            if not (rs <= kr < rs + 8):
                continue
            roff = kr - qr + 7
            g = rpb[:, roff][:, rel]
            out[:, ki * 64:(ki + 1) * 64, qi * 64:(qi + 1) * 64] = np.where(ok[None], g, np.float32(NEGB))
    return out


def host_na_table(rpb, kind, half):
    nt = na_ntypes(kind)
    out = np.empty((8, 128, nt, 128), dtype=np.float32)
    for i in range(NA_INT):
        dlt = 2 - i
        out[:, :, i, :] = _na_tile(rpb, 64, (16, 17), (16 + 2 * dlt, 17 + 2 * dlt))
    for n, (j, slot) in enumerate(na_special_pairs(kind)):
        if kind == "s":
            R = 32
            qrows = (2 * j, 2 * j + 1)
            krows = (2 * slot, 2 * slot + 1)
        else:
            R = 64
            if half == 0:
                qrows = (2 * j, 2 * j + 1)
                krows = (2 * slot, 2 * slot + 1)
            else:
                qrows = (32 + 2 * j, 33 + 2 * j)
                if slot < 16:
                    krows = (32 + 2 * slot, 33 + 2 * slot)
                else:
                    g = slot - 8
                    krows = (2 * g, 2 * g + 1)
        out[:, :, NA_INT + n, :] = _na_tile(rpb, R, qrows, krows)
    return out.reshape(8, 128, nt * 128)


class Buf:
    __slots__ = ("name", "w", "r", "dsem", "dcnt", "dma")

    def __init__(self, name):
        self.name = name
        self.w = None
        self.r = {}
        self.dsem = {}
        self.dcnt = {}
        self.dma = False


class Ctx:
    def __init__(self, nc, es):
        self.nc = nc
        self.es = es
        self.eng = {"pe": nc.tensor, "act": nc.scalar, "dve": nc.vector, "pool": nc.gpsimd, "sp": nc.sync}
        self.sem = {e: es.enter_context(nc.semaphore("s_" + e)) for e in self.eng}
        self.cnt = {e: 0 for e in self.eng}
        self.seen = {e: {} for e in self.eng}
        self.semname = {}
        self.dma_toks = []
        self.nsem = 0
        self.defer = None
        self.snap = {}

    def buf(self, name, dma=False):
        b = Buf(name)
        b.dma = dma
        return b

    def _wait(self, e, tok, raw):
        if tok is None:
            return
        te, sem, val = tok
        if te == e:
            if e == "pe" or e == "sp" or (not raw and e != "pool"):
                return
        key = id(sem)
        if self.seen[e].get(key, 0) >= val:
            return
        self.seen[e][key] = val
        snap = self.snap.get((key, val))
        if snap is not None:
            se = self.seen[e]
            for k_, v_ in snap.items():
                if se.get(k_, 0) < v_:
                    se[k_] = v_
        if self.defer is not None:
            self.defer.append((sem, val))
        else:
            self.eng[e].wait_ge(sem, val)

    def _sync(self, e, reads, writes):
        for b in reads:
            self._wait(e, b.w, True)
        for b in writes:
            self._wait(e, b.w, False)
            for t in b.r.values():
                self._wait(e, t, False)

    def _mark(self, tok, reads, writes):
        for b in reads:
            b.r[id(tok[1])] = tok
        for b in writes:
            b.w = tok
            b.r = {}

    def op(self, e, fns, reads=(), writes=()):
        self.defer = []
        self._sync(e, reads, writes)
        pend, self.defer = self.defer, None
        for (sem_, val_) in pend[:-1]:
            self.eng[e].wait_ge(sem_, val_)
        if not isinstance(fns, (list, tuple)):
            fns = [fns]
        ins = None
        for i_, f in enumerate(fns):
            ins = f()
            if i_ == 0 and pend:
                ins._wait_ge(pend[-1][0], pend[-1][1])
        self.cnt[e] += 1
        ins.then_inc(self.sem[e], 1)
        tok = (e, self.sem[e], self.cnt[e])
        if e != "pool":
            self.snap[(id(self.sem[e]), self.cnt[e])] = dict(self.seen[e])
        self._mark(tok, reads, writes)
        return tok

    def dma(self, q, out_ap, in_ap, reads, writes, sembuf, waw=True, track=True):
        att = (q == "sp")
        if att:
            self.defer = []
        if waw:
            self._sync(q, reads, writes)
        else:
            self._sync(q, reads, [])
        pend = []
        if att:
            pend, self.defer = self.defer, None
            for (sem_, val_) in pend[:-1]:
                self.eng[q].wait_ge(sem_, val_)
        cls = "sw" if q == "pool" else "hw"
        if cls not in sembuf.dsem:
            sembuf.dsem[cls] = self.es.enter_context(self.nc.semaphore("d%s_%s_%d" % (cls, sembuf.name, self.nsem)))
            sembuf.dcnt[cls] = 0
            self.nsem += 1
        sem = sembuf.dsem[cls]
        dins = self.eng[q].dma_start(out=out_ap, in_=in_ap)
        if pend:
            dins._wait_ge(pend[-1][0], pend[-1][1])
        dins.then_inc(sem, 16)
        sembuf.dcnt[cls] += 16
        tok = ("dma", sem, sembuf.dcnt[cls])
        self._mark(tok, reads, writes)
        if track:
            self.dma_toks.append(tok)
        return tok

    def barrier(self):
        toks = [(e, self.sem[e], self.cnt[e]) for e in self.eng if self.cnt[e] > 0]
        last = {}
        for t in self.dma_toks:
            k = id(t[1])
            if k not in last or last[k][2] < t[2]:
                last[k] = t
        toks += list(last.values())
        self.dma_toks = list(last.values())
        for e in self.eng:
            for t in toks:
                if t[0] == e:
                    continue
                key = id(t[1])
                if self.seen[e].get(key, 0) >= t[2]:
                    continue
                self.seen[e][key] = t[2]
                self.eng[e].wait_ge(t[1], t[2])


def build_program(n_sample=4, do_prompt=True, debug=False):
    nc = bass.Bass("TRN2", target_bir_lowering=False)
    units = ["s"] * n_sample + (["p"] if do_prompt else [])
    nunits = len(units)
    n_in_rows = nunits * UT + (HALO if do_prompt else 0)

    def din(name, shape, dt=F32):
        return nc.dram_tensor(name, list(shape), dt, kind="ExternalInput").ap()

    def dscr(name, shape, dt=BF16):
        return nc.dram_tensor(name, list(shape), dt, kind="Internal").ap()

    xin = din("xin", [n_in_rows, D])
    w_in = din("w_in", [D, 3 * D])
    w_out = din("w_out", [D, D])
    w_gate = din("w_gate", [D, DFF])
    w_up = din("w_up", [D, DFF])
    w_down = din("w_down", [DFF, D])
    gv = din("gv", [4, D])
    bdil = din("bdil", [8, 128, N_DIL_T * 128])
    nts, ntp = na_ntypes("s"), na_ntypes("p")
    bna_s = din("bna_s", [8, 128, nts * 128])
    bna_p = din("bna_p", [8, 128, ntp * 128])
    gates_in = din("gates", [128, 2])
    yout = nc.dram_tensor("yout", [nunits * UT, D], F32, kind="ExternalOutput").ap()

    w_in_b = dscr("w_in_b", [D, 3 * D])
    w_out_b = dscr("w_out_b", [D, D])
    w_gate_b = dscr("w_gate_b", [D, DFF])
    w_up_b = dscr("w_up_b", [D, DFF])
    w_down_b = dscr("w_down_b", [DFF, D])
    wdil = dscr("wdil", [8, 128, N_DIL_T * 128])
    wna_s = dscr("wna_s", [8, 128, nts * 128])
    wna_p = dscr("wna_p", [8, 128, ntp * 128])
    NTMAX = max(nts, ntp)

    es = contextlib.ExitStack()
    with es:
        cx = Ctx(nc, es)
        pe, act, dve, pool, sp = nc.tensor, nc.scalar, nc.vector, nc.gpsimd, nc.sync

        def sb(name, shape, dt):
            return es.enter_context(nc.sbuf_tensor(name, list(shape), dt))

        def ps(name, shape, dt):
            return es.enter_context(nc.psum_tensor(name, list(shape), dt))

        gb = sb("gb", [128, 4, D], F32)
        ident = sb("ident", [128, 128], BF16)
        identf = sb("identf", [128, 128], F32)
        gates = sb("gates_sb", [128, 2], F32)
        xt = [sb("xt%d" % i, [128, D], F32) for i in range(NXT)]
        hb = [sb("hb%d" % i, [128, D], BF16) for i in range(NHB)]
        o_sb = sb("o_sb", [128, 16, D], BF16)
        junk = sb("junk", [128, D], BF16)
        yt = [sb("yt%d" % i, [128, D], F32) for i in range(2)]
        stat = sb("stat", [128, 64], F32)
        eps_t = sb("eps_t", [128, 1], F32)
        rden = [sb("rden%d" % i, [128, 4], F32) for i in range(2)]
        REG_ELEMS = 65536
        reg = sb("region", [128, REG_ELEMS], BF16)
        regpos = [0]

        def carve(shape_free, dt):
            n = int(np.prod(shape_free))
            ne = n * (2 if dt == F32 else 1)
            off = regpos[0]
            off = (off + 1) // 2 * 2
            assert off + ne <= REG_ELEMS, (off, ne)
            regpos[0] = off + ne
            ap = reg[:, off:off + ne]
            if dt == F32:
                ap = ap.bitcast(F32)
            if len(shape_free) == 2:
                ap = ap.rearrange("p (a b) -> p a b", a=shape_free[0])
            elif len(shape_free) == 3:
                ap = ap.rearrange("p (a b c) -> p a b c", a=shape_free[0], b=shape_free[1])
            return ap

        regpos[0] = 0
        hnT = carve([8, UT + HALO], BF16)
        Vflat = carve([24 * 8, 65], BF16)
        Vsb = Vflat.rearrange("p (a b) c -> p a b c", a=24)
        qtm_off = (regpos[0] + 1) // 2 * 2
        QTm = [carve([UT], BF16) for _ in range(2)]
        KT1 = carve([UT + HALO], BF16)
        kt_end = regpos[0]
        wv_sb = carve([8, 512], BF16)
        wq_sb = [carve([8, 128], BF16) for _ in range(2)]
        wk_sb = [carve([8, 128], BF16) for _ in range(2)]
        wtab = [carve([NTMAX * 128], BF16) for _ in range(2)]
        pexp = [carve([512], BF16) for _ in range(NPE)]
        pw = [carve([512], BF16) for _ in range(NPW)]
        endA = regpos[0]
        regpos[0] = 0
        hnT_f32 = carve([8, (UT + HALO) // 2], F32)
        regpos[0] = qtm_off
        btmp = carve([NTMAX * 128], F32)
        assert regpos[0] <= kt_end
        regpos[0] = 0
        wout_sb = carve([8, D], BF16)
        wd_sb = carve([NJ, D], BF16)
        x1 = carve([4, D], F32)
        mixT = carve([8, 512], BF16)
        hn2T = carve([8, 512], BF16)
        hT = carve([NJ, 512], BF16)
        wg_sb = [carve([8, 128], BF16) for _ in range(2)]
        wu_sb = [carve([8, 128], BF16) for _ in range(2)]
        sg = [carve([512], F32) for _ in range(2)]
        endB = regpos[0]

        NMM = 6
        mm = [ps("mm%d" % i, [128, 512], F32) for i in range(NMM)]
        tp = [m[:, :].bitcast(BF16).rearrange("p (c n) -> p c n", c=8) for m in mm]
        pv = [ps("pv%d" % i, [128, 4, 128], F32) for i in range(2)]

        B = cx.buf
        b_gb = B("gb", dma=True)
        b_ident = B("ident")
        b_gates = B("gates", dma=True)
        b_xt = [B("xt%d" % i, dma=True) for i in range(NXT)]
        b_hb = [B("hb%d" % i) for i in range(NHB)]
        b_o = [B("o%d" % i) for i in range(16)]
        b_junk = B("junk")
        b_yt = [B("yt%d" % i, dma=True) for i in range(2)]
        b_stats = [B("stat%d" % i) for i in range(8)]
        b_rden = [B("rden%d" % i) for i in range(2)]
        b_hnT = [B("hnT%d" % i) for i in range(6)]
        b_V = [B("V%d" % i) for i in range(24)]
        b_QTm = [B("QTm%d" % i) for i in range(2)]
        b_KT1 = B("KT1")
        b_wv = B("wv", dma=True)
        b_wq = [B("wq%d" % i, dma=True) for i in range(2)]
        b_wk = [B("wk%d" % i, dma=True) for i in range(2)]
        b_wtab = [B("wtab%d" % i, dma=True) for i in range(2)]
        b_pexp = [B("pexp%d" % i) for i in range(NPE)]
        b_pw = [B("pw%d" % i) for i in range(NPW)]
        b_tabld = B("tabld", dma=True)
        b_eps = B("eps")
        b_wout = B("wout", dma=True)
        b_wd = B("wd", dma=True)
        b_x1 = [B("x1_%d" % i) for i in range(4)]
        b_mixT = B("mixT")
        b_hn2T = B("hn2T")
        b_hT = [B("hT%d" % i) for i in range(NJ)]
        b_wg = [B("wg%d" % i, dma=True) for i in range(2)]
        b_wu = [B("wu%d" % i, dma=True) for i in range(2)]
        b_sg = [B("sg%d" % i) for i in range(2)]
        b_mm = [B("mm%d" % i) for i in range(NMM)]
        b_tp = b_mm
        b_pv = [B("pv%d" % i) for i in range(2)]
        b_scr = {n: B(n, dma=True) for n in ("w_out_b", "w_gate_b", "w_up_b", "w_down_b")}
        b_win = [B("w_in_b%d" % i, dma=True) for i in range(6)]
        gate_t = sb("gate_t", [128, 2], F32)
        b_gate_t = B("gate_t")
        b_wdil = [B("wdil%d" % h, dma=True) for h in range(8)]
        b_wnas = [B("wnas%d" % h, dma=True) for h in range(8)]
        b_wnap = [B("wnap%d" % h, dma=True) for h in range(8)]

        rr = {"mm": 0, "tp": 0, "xt": 0, "hb": 0, "yt": 0, "pe": 0, "pw": 0, "pv": 0, "rden": 0, "wt": 0,
              "qk": 0, "wgu": 0, "sg": 0, "st": 0}

        def nxt(key, n):
            i = rr[key]
            rr[key] = (i + 1) % n
            return i

        def stat_slot():
            i = nxt("st", 8)
            return i * 8

        def cast_weight(name, src, dst):
            nr = src.shape[0]
            for r0 in range(0, nr, 256):
                r1 = min(nr, r0 + 256)
                cx.dma("pool", dst[r0:r1, :], src[r0:r1, :], [], [b_scr[name]], b_scr[name], waw=False, track=False)

        def cast_win_block(i):
            for r0 in range(0, D, 256):
                cx.dma("pool", w_in_b[r0:r0 + 256, i * 512:(i + 1) * 512], w_in[r0:r0 + 256, i * 512:(i + 1) * 512],
                       [], [b_win[i]], b_win[i], waw=False, track=False)

        def pool_gate(bufs):
            cx.op("pool", lambda: pool.memset(gate_t[:], 0.0), bufs, [b_gate_t])

        for i in (2, 0, 1):
            cast_win_block(i)
        for i in range(4):
            cx.dma("sp", gb[:, i, :], gv[i:i + 1, :].partition_broadcast(128), [], [b_gb], b_gb, waw=False)
        cx.dma("sp", gates[:], gates_in[:, :], [], [b_gates], b_gates)
        cx.op("pool", lambda: pool.memset(identf[:], 1.0), [], [b_ident])
        cx.op("pool", lambda: pool.affine_select(out=identf[:], in_=identf[:], pattern=[[-1, 128]],
                                                compare_op=ALU.is_equal, fill=0.0, base=0, channel_multiplier=1),
              [b_ident], [b_ident])
        cx.op("dve", lambda: dve.tensor_copy(ident[:], identf[:]), [b_ident], [b_ident])
        cx.op("dve", lambda: dve.memset(eps_t[:], EPS), [], [b_eps])

        b_alias = [b_QTm[0], b_QTm[1], b_KT1]

        def build_table_head(src, dst, bl, ncols, h):
            cx.dma("sp", btmp[:, 0:ncols], src[h], [], b_alias, b_tabld)
            wb = wtab[h % 2]
            cx.op("act", (lambda wb=wb: act.activation(out=wb[:, 0:ncols], in_=btmp[:, 0:ncols], func=AF.Exp)),
                  b_alias, [b_wtab[h % 2]])
            cx.dma("sp", dst[h], wb[:, 0:ncols], [b_wtab[h % 2]], [bl[h]], bl[h])

        built = {"s": False, "p": False, "casts": False}
        cast_jobs = []
        cast_gate = [None]
        def rms_sd(src_ap, b_src, ncols, st0):
            b_stat = b_stats[st0 // 8]
            cx.op("act", lambda: act.activation(out=junk[:, 0:ncols], in_=src_ap, func=AF.Square,
                                                accum_out=stat[:, st0:st0 + 1]),
                  [b_src], [b_junk, b_stat])
            cx.op("act", lambda: act.activation(out=stat[:, st0 + 1:st0 + 2], in_=stat[:, st0:st0 + 1],
                                                func=AF.Sqrt, scale=1.0 / ncols, bias=eps_t[:, 0:1]),
                  [b_stat, b_eps], [b_stat])
            return stat[:, st0 + 1:st0 + 2]

        def rms_apply(out_ap, b_out, src_ap, b_src, sd, st0, g_ap):
            b_stat = b_stats[st0 // 8]
            cx.op("dve", lambda: dve.reciprocal(out=stat[:, st0 + 2:st0 + 3], in_=sd), [b_stat], [b_stat])
            cx.op("dve", lambda: dve.scalar_tensor_tensor(out=out_ap, in0=src_ap, scalar=stat[:, st0 + 2:st0 + 3], in1=g_ap,
                                                          op0=ALU.mult, op1=ALU.mult),
                  [b_src, b_stat, b_gb], [b_out])

        def transpose_to(src_bf, b_src, dstT, b_dst, col0):
            ti = nxt("mm", NMM)
            cx.op("pe", [(lambda c=c: pe.transpose(tp[ti][:, c, :], src_bf[:, c * 128:(c + 1) * 128], ident[:]))
                         for c in range(8)],
                  [b_src, b_ident], [b_tp[ti]])
            cx.op("act", lambda: act.copy(out=dstT[:, :, col0:col0 + 128], in_=tp[ti]),
                  [b_tp[ti]], [b_dst])

        def norm_tile_A(src_ap, b_src, gidx):
            st0 = stat_slot()
            sd = rms_sd(src_ap, b_src, D, st0)
            hi = nxt("hb", NHB)
            rms_apply(hb[hi][:], b_hb[hi], src_ap, b_src, sd, st0, gb[:, gidx, :])
            return hi

        def norm_tile_B(hi, dstT, b_dst, col0):
            transpose_to(hb[hi], b_hb[hi], dstT, b_dst, col0)

        def load_w_chunk(dst, b_dst, src_scr, b_src, col0, ncol, q="sp"):
            if src_scr is w_in_b:
                b_src = b_win[col0 // 512]
            cx.dma(q, dst[:, :, 0:ncol], src_scr[:, col0:col0 + ncol].rearrange("(c p) n -> p c n", p=128),
                   [b_src], [b_dst], b_dst)

        def proj_T(w_sb, b_w, srcT, b_srcs, col0, n, dst_ap, b_dst):
            mi = nxt("mm", NMM)
            cx.op("pe", [(lambda kc=kc: pe.matmul(mm[mi][:, 0:n], lhsT=w_sb[:, kc, :], rhs=srcT[:, kc, col0:col0 + n],
                                                  start=(kc == 0), stop=(kc == 7))) for kc in range(8)],
                  [b_w] + b_srcs, [b_mm[mi]])
            cx.op("dve", lambda: dve.tensor_copy(dst_ap, mm[mi][:, 0:n]), [b_mm[mi]], [b_dst])

        PV_DEPTH = 4
        pvq = []

        def pv_emit(ent):
            (k, ja, jb, pwi, pvi, j0, hv, hcol, first, last) = ent
            fns = []
            for n_, j in enumerate(range(ja, jb)):
                st_flag = first and n_ == 0
                fns.append(lambda n_=n_, j=j, st_flag=st_flag: pe.matmul(
                    pv[pvi][:, j - j0, 0:65], lhsT=pw[pwi][:, n_ * 128:(n_ + 1) * 128],
                    rhs=Vsb[:, k, hv, :], start=st_flag, stop=False, skip_group_check=True))
            cx.op("pe", fns, [b_pw[pwi], b_V[k]], [b_pv[pvi]])
            if last:
                ri = nxt("rden", 2)
                cx.op("dve", lambda: dve.reciprocal(out=rden[ri][:], in_=pv[pvi][:, :, 64]),
                      [b_pv[pvi]], [b_rden[ri]])
                cx.op("dve", lambda: dve.tensor_tensor(out=o_sb[:, j0:j0 + 4, hcol:hcol + 64], in0=pv[pvi][:, :, 0:64],
                                                       in1=rden[ri][:].unsqueeze(2).to_broadcast([128, 4, 64]),
                                                       op=ALU.mult),
                      [b_pv[pvi], b_rden[ri]], [b_o[j] for j in range(j0, j0 + 4)])

        def pv_flush(keep=0):
            while len(pvq) > keep:
                pv_emit(pvq.pop(0))

        def attention_head(kind, chunks, qt_ap, kt_ap, b_qt, b_kt, hcol, hv, wt_ap, b_wt):
            for (j0, steps) in chunks:
                pvi = nxt("pv", 2)
                ns = len(steps)
                for si_, (k, ja, jb, ti, gate) in enumerate(steps):
                    n = (jb - ja) * 128
                    mi = nxt("mm", NMM)
                    cx.op("pe", lambda: pe.matmul(mm[mi][:, 0:n], lhsT=kt_ap[:, k * 128:(k + 1) * 128],
                                                  rhs=qt_ap[:, ja * 128:jb * 128], start=True, stop=True),
                          [b_kt, b_qt], [b_mm[mi]])
                    pv_flush(PV_DEPTH)
                    pei = nxt("pe", NPE)
                    cx.op("act", lambda: act.activation(out=pexp[pei][:, 0:n], in_=mm[mi][:, 0:n], func=AF.Exp,
                                                        scale=0.125),
                          [b_mm[mi]], [b_pexp[pei]])
                    pwi = nxt("pw", NPW)
                    if gate is None:
                        cx.op("dve", lambda: dve.tensor_tensor(out=pw[pwi][:, 0:n], in0=pexp[pei][:, 0:n],
                                                               in1=wt_ap[:, ti * 128:ti * 128 + n], op=ALU.mult),
                              [b_pexp[pei], b_wt], [b_pw[pwi]])
                    else:
                        cx.op("dve", lambda: dve.scalar_tensor_tensor(out=pw[pwi][:, 0:n], in0=pexp[pei][:, 0:n],
                                                                      scalar=gates[:, gate:gate + 1],
                                                                      in1=wt_ap[:, ti * 128:ti * 128 + n],
                                                                      op0=ALU.mult, op1=ALU.mult),
                              [b_pexp[pei], b_wt, b_gates], [b_pw[pwi]])
                    pvq.append((k, ja, jb, pwi, pvi, j0, hv, hcol, si_ == 0, si_ == ns - 1))

        steps_cache = {("na", "s"): na_steps("s"), ("na", "p"): na_steps("p"),
                       ("dil", "s"): dil_steps("s"), ("dil", "p"): dil_steps("p")}
        for u, kind in enumerate(units):
            row0 = u * UT
            nkt = 24 if kind == "p" else 16
            ntok = nkt * 128

            def xrows(t):
                if t < 16:
                    return row0 + t * 128
                return nunits * UT + (t - 16) * 128

            def vproj(t):
                mi = nxt("mm", NMM)
                cx.op("pe", [(lambda kc=kc: pe.matmul(mm[mi][:, :], lhsT=hnT[:, kc, t * 128:(t + 1) * 128],
                                                      rhs=wv_sb[:, kc, :], start=(kc == 0), stop=(kc == 7)))
                             for kc in range(8)],
                      [b_wv, b_hnT[t // 4]], [b_mm[mi]])
                cx.op("dve", lambda: dve.tensor_copy(Vsb[:, t, :, 0:64],
                                                     mm[mi][:, :].rearrange("p (h d) -> p h d", h=8)),
                      [b_mm[mi]], [b_V[t]])

            if u > 0:
                cx.barrier()
            cx.op("dve", lambda: dve.memset(Vflat[:, :, 64:65], 1.0), [], b_V)
            wq_ = "pool" if u == 0 else "sp"
            load_w_chunk(wv_sb, b_wv, w_in_b, None, 1024, 512, q=wq_)
            qi0 = nxt("qk", 2)
            load_w_chunk(wq_sb[qi0], b_wq[qi0], w_in_b, None, 0, 128, q=wq_)
            load_w_chunk(wk_sb[qi0], b_wk[qi0], w_in_b, None, 512, 128, q=wq_)
            early_qk = built[kind]

            def zero_q_halves():
                cx.op("pool", lambda: pool.memset(QTm[0][64:128, :], 0.0), [], [b_QTm[0]])
                cx.op("pool", lambda: pool.memset(QTm[1][0:64, :], 0.0), [], [b_QTm[1]])

            def qproj(qi, tq):
                mi = nxt("mm", NMM)
                cx.op("pe", [(lambda kc=kc: pe.matmul(mm[mi][:, :], lhsT=wq_sb[qi][:, kc, :],
                                                      rhs=hnT[:, kc, tq * 512:(tq + 1) * 512],
                                                      start=(kc == 0), stop=(kc == 7))) for kc in range(8)],
                      [b_wq[qi], b_hnT[tq]], [b_mm[mi]])
                cx.op("dve", lambda: dve.tensor_copy(QTm[0][0:64, tq * 512:(tq + 1) * 512], mm[mi][0:64, :]),
                      [b_mm[mi]], [b_QTm[0]])
                cx.op("dve", lambda: dve.tensor_copy(QTm[1][64:128, tq * 512:(tq + 1) * 512], mm[mi][64:128, :]),
                      [b_mm[mi]], [b_QTm[1]])

            def kproj(qi, tk):
                proj_T(wk_sb[qi], b_wk[qi], hnT, [b_hnT[tk]], tk * 512, 512, KT1[:, tk * 512:(tk + 1) * 512], b_KT1)

            if early_qk:
                zero_q_halves()
            def after_tile(t):
                norm_tile_B(prev[0], hnT, b_hnT[t // 4], t * 128)
                vproj(t)
                if early_qk and t % 4 == 3:
                    if t // 4 < 4:
                        qproj(qi0, t // 4)
                    kproj(qi0, t // 4)
                if not built[kind] and t % 2 == 1 and t < 16:
                    if kind == "s":
                        build_table_head(bna_s, wna_s, b_wnas, nts * 128, t // 2)
                    else:
                        build_table_head(bna_p, wna_p, b_wnap, ntp * 128, t // 2)

            prev = None
            for t in range(nkt):
                xi = nxt("xt", NXT)
                r0 = xrows(t)
                cx.dma("sp", xt[xi][:], xin[r0:r0 + 128, :], [], [b_xt[xi]], b_xt[xi])
                hi = norm_tile_A(xt[xi][:], b_xt[xi], 0)
                if prev is not None:
                    after_tile(prev[1])
                prev = (hi, t)
            after_tile(prev[1])
            if not built[kind]:
                last_tab = b_wnas[7] if kind == "s" else b_wnap[7]
                built[kind] = True
                if not built["casts"]:
                    def _rows(name, src, dst, r0, r1):
                        def f():
                            for a in range(r0, r1, 256):
                                b_ = min(r1, a + 256)
                                cx.dma("pool", dst[a:b_, :], src[a:b_, :], [], [b_scr[name]], b_scr[name],
                                       waw=False, track=False)
                        return f

                    def _dil(h0):
                        def f():
                            for h in range(h0, h0 + 4):
                                cx.dma("pool", wdil[h], bdil[h], [], [b_wdil[h]], b_wdil[h], track=False)
                        return f

                    cast_jobs.extend([
                        (lambda: [cast_win_block(i) for i in (5, 3, 4)]),
                        _dil(0), _dil(4),
                        _rows("w_out_b", w_out, w_out_b, 0, D),
                        _rows("w_gate_b", w_gate, w_gate_b, 0, 512), _rows("w_gate_b", w_gate, w_gate_b, 512, D),
                        _rows("w_up_b", w_up, w_up_b, 0, 512), _rows("w_up_b", w_up, w_up_b, 512, D),
                        _rows("w_down_b", w_down, w_down_b, 0, 1408), _rows("w_down_b", w_down, w_down_b, 1408, DFF),
                    ])
                    cast_gate[0] = last_tab
                    built["casts"] = True
            if not early_qk:
                zero_q_halves()
            def release_casts(njobs, gate_bufs):
                if cast_jobs:
                    pool_gate(gate_bufs)
                    for _ in range(njobs):
                        if cast_jobs:
                            cast_jobs.pop(0)()

            if cast_jobs:
                release_casts(1, [cast_gate[0]])

            pairs = [(G, c) for G in range(2) for c in range(4)]
            pair_qi = {0: qi0}

            def load_pair(pi):
                G, c = pairs[pi]
                qi = nxt("qk", 2)
                load_w_chunk(wq_sb[qi], b_wq[qi], w_in_b, None, 1536 * G + c * 128, 128)
                load_w_chunk(wk_sb[qi], b_wk[qi], w_in_b, None, 1536 * G + 512 + c * 128, 128)
                pair_qi[pi] = qi

            head_wt = {}

            def load_wtab(hn):
                G, h = hn // 8, hn % 8
                wi = nxt("wt", 2)
                if G == 0:
                    src, bsrc, ncols = ((wna_s, b_wnas, nts * 128) if kind == "s" else (wna_p, b_wnap, ntp * 128))
                else:
                    src, bsrc, ncols = wdil, b_wdil, N_DIL_T * 128
                cx.dma("sp", wtab[wi][:, 0:ncols], src[h], [bsrc[h]], [b_wtab[wi]], b_wtab[wi])
                head_wt[hn] = wi

            bg_tab = (kind == "s" and not built["p"] and u + 1 < nunits and units[u + 1] == "p")
            PC = ntp * 128 // 16
            stage_f = hnT_f32[:, :, UT // 2:UT // 2 + PC]
            stage_b = hnT[:, :, UT + 2 * PC:UT + 3 * PC]
            b_stage = [b_hnT[4], b_hnT[5]]

            def bg_load(r):
                h, hf_ = r // 2, r % 2
                cx.dma("sp", stage_f, bna_p[h].rearrange("p (a b) -> p a b", a=16)[:, hf_ * 8:(hf_ + 1) * 8, :],
                       [], b_stage, b_tabld)

            def bg_exp_store(r):
                h, hf_ = r // 2, r % 2
                cx.op("act", lambda: act.activation(out=stage_b, in_=stage_f, func=AF.Exp), b_stage, b_stage)
                cx.dma("sp", wna_p[h].rearrange("p (a b) -> p a b", a=16)[:, hf_ * 8:(hf_ + 1) * 8, :], stage_b,
                       b_stage, [b_wnap[h]], b_wnap[h], waw=False)

            def bg_round(r):
                if bg_tab:
                    if r >= 1:
                        bg_exp_store(r - 1)
                    if r < 16:
                        bg_load(r)

            load_wtab(0)
            for pi, (G, c) in enumerate(pairs):
                bg_round(2 * pi)
                cb = 1536 * G
                gname = "na" if G == 0 else "dil"
                chunks = steps_cache[(gname, kind)]
                if G == 1 and c == 0:
                    pv_flush()
                    load_w_chunk(wv_sb, b_wv, w_in_b, None, cb + 1024, 512)
                    for t in range(nkt):
                        vproj(t)
                qi = pair_qi[pi]
                if not (pi == 0 and early_qk):
                    for tq in range(4):
                        qproj(qi, tq)
                    for tk in range(ntok // 512):
                        kproj(qi, tk)
                if pi >= 1:
                    release_casts(1 if pi < 6 else 2, [b_KT1])
                if pi + 1 < len(pairs):
                    load_pair(pi + 1)
                for hl in range(2):
                    hn = 2 * pi + hl
                    h = hn % 8
                    if hn + 1 < 16:
                        load_wtab(hn + 1)
                    wi = head_wt[hn]
                    if hl == 1:
                        bg_round(2 * pi + 1)
                    attention_head(kind, chunks, QTm[hl], KT1, b_QTm[hl], b_KT1,
                                   G * 512 + h * 64, h, wtab[wi], b_wtab[wi])
            if bg_tab:
                bg_round(16)
                built["p"] = True
            pv_flush()
            cx.barrier()
            cx.dma("sp", wout_sb[:], w_out_b.rearrange("(c p) n -> p c n", p=128), [b_scr["w_out_b"]], [b_wout], b_wout)
            cx.dma("pool", wd_sb[:], w_down_b.rearrange("(c p) n -> p c n", p=128), [b_scr["w_down_b"]], [b_wd], b_wd)

            def mix_A(qd, tl):
                j = qd * 4 + tl
                st0 = stat_slot()
                sa_ = rms_sd(o_sb[:, j, 0:512], b_o[j], 512, st0)
                st1 = stat_slot()
                sb_ = rms_sd(o_sb[:, j, 512:1024], b_o[j], 512, st1)
                hi = nxt("hb", NHB)
                rms_apply(hb[hi][:, 0:512], b_hb[hi], o_sb[:, j, 0:512], b_o[j], sa_, st0, gb[:, 1, 0:512])
                rms_apply(hb[hi][:, 512:1024], b_hb[hi], o_sb[:, j, 512:1024], b_o[j], sb_, st1, gb[:, 1, 512:1024])
                return hi

            def mix_phase(qd):
                prev = None
                for tl in range(4):
                    hi = mix_A(qd, tl)
                    if prev is not None:
                        norm_tile_B(prev[0], mixT, b_mixT, prev[1] * 128)
                    prev = (hi, tl)
                norm_tile_B(prev[0], mixT, b_mixT, prev[1] * 128)

            def outproj_phase(qd):
                for tl in range(4):
                    j = qd * 4 + tl
                    xi = nxt("xt", NXT)
                    r0 = row0 + j * 128
                    cx.dma("sp", xt[xi][:], xin[r0:r0 + 128, :], [], [b_xt[xi]], b_xt[xi])
                    for hf in range(2):
                        mi = nxt("mm", NMM)
                        cx.op("pe", [(lambda kc=kc: pe.matmul(mm[mi][:, :], lhsT=mixT[:, kc, tl * 128:(tl + 1) * 128],
                                                              rhs=wout_sb[:, kc, hf * 512:(hf + 1) * 512],
                                                              start=(kc == 0), stop=(kc == 7))) for kc in range(8)],
                              [b_mixT, b_wout], [b_mm[mi]])
                        cx.op("dve", lambda: dve.tensor_tensor(out=x1[:, tl, hf * 512:(hf + 1) * 512],
                                                               in0=mm[mi][:, :], in1=xt[xi][:, hf * 512:(hf + 1) * 512],
                                                               op=ALU.add),
                              [b_mm[mi], b_xt[xi]], [b_x1[tl]])
                prev = None
                for tl in range(4):
                    hi = norm_tile_A(x1[:, tl, :], b_x1[tl], 2)
                    if prev is not None:
                        norm_tile_B(prev[0], hn2T, b_hn2T, prev[1] * 128)
                    prev = (hi, tl)
                norm_tile_B(prev[0], hn2T, b_hn2T, prev[1] * 128)

            def ffn_phase(qd, mix_next=False):
                mix_hi = {}
                for jf in range(NJ):
                    if mix_next:
                        if jf % 5 == 1 and jf // 5 < 4:
                            mix_hi[jf // 5] = mix_A(qd + 1, jf // 5)
                        if jf % 5 == 4 and jf // 5 < 4:
                            norm_tile_B(mix_hi[jf // 5], mixT, b_mixT, (jf // 5) * 128)
                    wi = nxt("wgu", 2)
                    load_w_chunk(wg_sb[wi], b_wg[wi], w_gate_b, b_scr["w_gate_b"], jf * 128, 128)
                    load_w_chunk(wu_sb[wi], b_wu[wi], w_up_b, b_scr["w_up_b"], jf * 128, 128)
                    mg = nxt("mm", NMM)
                    cx.op("pe", [(lambda kc=kc: pe.matmul(mm[mg][:, :], lhsT=wg_sb[wi][:, kc, :], rhs=hn2T[:, kc, :],
                                                          start=(kc == 0), stop=(kc == 7))) for kc in range(8)],
                          [b_wg[wi], b_hn2T], [b_mm[mg]])
                    mu = nxt("mm", NMM)
                    cx.op("pe", [(lambda kc=kc: pe.matmul(mm[mu][:, :], lhsT=wu_sb[wi][:, kc, :], rhs=hn2T[:, kc, :],
                                                          start=(kc == 0), stop=(kc == 7))) for kc in range(8)],
                          [b_wu[wi], b_hn2T], [b_mm[mu]])
                    si = nxt("sg", 2)
                    cx.op("act", lambda: act.activation(out=sg[si][:], in_=mm[mg][:, :], func=AF.Silu),
                          [b_mm[mg]], [b_sg[si]])
                    cx.op("dve", lambda: dve.tensor_tensor(out=hT[:, jf, :], in0=mm[mu][:, :], in1=sg[si][:], op=ALU.mult),
                          [b_mm[mu], b_sg[si]], [b_hT[jf]])

            def down_phase(qd):
                def final_norm(yi, j):
                    st0 = stat_slot()
                    sd = rms_sd(yt[yi][:], b_yt[yi], D, st0)
                    rms_apply(yt[yi][:], b_yt[yi], yt[yi][:], b_yt[yi], sd, st0, gb[:, 3, :])
                    r0 = row0 + j * 128
                    cx.dma("pool", yout[r0:r0 + 128, :], yt[yi][:], [b_yt[yi]], [], b_yt[yi])

                prev = None
                for tl in range(4):
                    j = qd * 4 + tl
                    yi = nxt("yt", 2)
                    for hf in range(2):
                        mi = nxt("mm", NMM)
                        cx.op("pe", [(lambda jf=jf: pe.matmul(mm[mi][:, :], lhsT=hT[:, jf, tl * 128:(tl + 1) * 128],
                                                              rhs=wd_sb[:, jf, hf * 512:(hf + 1) * 512],
                                                              start=(jf == 0), stop=(jf == NJ - 1))) for jf in range(NJ)],
                              b_hT + [b_wd], [b_mm[mi]])
                        cx.op("dve", lambda: dve.tensor_tensor(out=yt[yi][:, hf * 512:(hf + 1) * 512], in0=mm[mi][:, :],
                                                               in1=x1[:, tl, hf * 512:(hf + 1) * 512], op=ALU.add),
                              [b_mm[mi], b_x1[tl]], [b_yt[yi]])
                    if prev is not None:
                        final_norm(*prev)
                    prev = (yi, j)
                final_norm(*prev)

            mix_phase(0)
            for qd in range(4):
                outproj_phase(qd)
                ffn_phase(qd, mix_next=(qd + 1 < 4))
                down_phase(qd)
        cx.barrier()
    return nc


_DIL_TABLE = None


def make_in_maps(x_prompt, x_sample, w_in, rpb, g_attn, g_na, g_dil, w_out, g_ffn, w_gate, w_up, w_down, g_final,
                 n_cores=8, n_sample=4, do_prompt=True):
    global _DIL_TABLE
    if _DIL_TABLE is None:
        _DIL_TABLE = host_dil_table()
    f = lambda a: np.ascontiguousarray(np.asarray(a, dtype=np.float32))
    gvec = np.stack([f(g_attn)[0], np.concatenate([f(g_na)[0], f(g_dil)[0]]), f(g_ffn)[0], f(g_final)], axis=0)
    rpb0 = f(rpb)[0]
    bna_s = host_na_table(rpb0, "s", 0)
    bna_p = [host_na_table(rpb0, "p", 0), host_na_table(rpb0, "p", 1)]
    xs = f(x_sample)
    xp = f(x_prompt)
    maps = []
    for c in range(n_cores):
        rows = [xs[n_sample * c + i] for i in range(n_sample)]
        half = c % 2
        if do_prompt:
            s = c // 2
            rows.append(xp[s, half * UT:(half + 1) * UT])
            rows.append(xp[s, UT:UT + HALO] if half == 0 else xp[s, HALO:UT])
        g = np.zeros((128, 2), np.float32)
        g[:, half] = 1.0
        maps.append({
            "xin": np.ascontiguousarray(np.concatenate(rows, axis=0)),
            "w_in": f(w_in)[0], "w_out": f(w_out)[0], "w_gate": f(w_gate)[0], "w_up": f(w_up)[0],
            "w_down": f(w_down)[0], "gv": gvec, "bdil": _DIL_TABLE, "bna_s": bna_s, "bna_p": bna_p[half],
            "gates": g,
        })
    return maps


_NC_CACHE = {}


def kernel(x_prompt, x_sample, w_in, rpb, g_attn, g_na, g_dil, w_out, g_ffn, w_gate, w_up, w_down, g_final):
    if "nc" not in _NC_CACHE:
        _NC_CACHE["nc"] = build_program(4, True)
    nc = _NC_CACHE["nc"]
    maps = make_in_maps(x_prompt, x_sample, w_in, rpb, g_attn, g_na, g_dil, w_out, g_ffn, w_gate, w_up, w_down, g_final)
    res = run_bass_kernel_spmd(nc, maps, core_ids=list(range(8)))
    y_sample = np.empty((32, 2048, D), np.float32)
    y_prompt = np.empty((4, 4096, D), np.float32)
    for c in range(8):
        y = np.asarray(res.results[c]["yout"], dtype=np.float32)
        for i in range(4):
            y_sample[4 * c + i] = y[i * UT:(i + 1) * UT]
        y_prompt[c // 2, (c % 2) * UT:(c % 2 + 1) * UT] = y[4 * UT:5 * UT]
    return (y_prompt, y_sample)
```
